# Optimizing a Trainium2 kernel written in Bass

```python
import math
import jax, jax.numpy as jnp
from jax import lax
import numpy as np

D_MODEL = 1024
BATCH = 4
SEQ = 4096
DEPTH = 2

CHUNK = 64
HEAD_DIM = 64
RET_HEADS = 6
SB_HEADS = 6
GLA_HEADS = 4
RET_W = RET_HEADS * HEAD_DIM
SB_W = SB_HEADS * HEAD_DIM
GLA_W = GLA_HEADS * HEAD_DIM
GLA_GATE_RANK = 16
GLA_TAU = 16.0
D_FF = 2816
FFN_CONV = 3
SB_BLOCK = 128
ROPE_BASE = 10000.0
EPS = 1e-6
IN_COLS = 4 * RET_W + 3 * SB_W + 4 * GLA_W + GLA_GATE_RANK

kernel_name = "hybrid_retention_stickbreak_gla_convffn"


def rmsnorm(x, g):
    xf = x.astype(jnp.float32)
    y = xf * lax.rsqrt(jnp.mean(xf * xf, axis=-1, keepdims=True) + EPS)
    return (y * g.astype(jnp.float32)).astype(x.dtype)


def split_heads(t, n_heads):
    b, s, _ = t.shape
    return t.reshape(b, s, n_heads, HEAD_DIM).transpose(0, 2, 1, 3)


def merge_heads(t):
    b, h, s, d = t.shape
    return t.transpose(0, 2, 1, 3).reshape(b, s, h * d)


def rotary(t):
    s, d = t.shape[2], t.shape[3]
    half = d // 2
    pos = jnp.arange(s, dtype=jnp.float32)
    freqs = ROPE_BASE ** (-jnp.arange(half, dtype=jnp.float32) / half)
    ang = pos[:, None] * freqs[None, :]
    cos, sin = jnp.cos(ang), jnp.sin(ang)
    tf = t.astype(jnp.float32)
    t1, t2 = tf[..., :half], tf[..., half:]
    return jnp.concatenate([t1 * cos - t2 * sin, t1 * sin + t2 * cos], axis=-1)


def head_groupnorm(o, g):
    h, d = o.shape[1], o.shape[3]
    mu = jnp.mean(o, axis=-1, keepdims=True)
    var = jnp.mean(jnp.square(o - mu), axis=-1, keepdims=True)
    return (o - mu) * lax.rsqrt(var + EPS) * g.astype(jnp.float32).reshape(h, 1, d)


def head_rmsnorm(o, g):
    h, d = o.shape[1], o.shape[3]
    return o * lax.rsqrt(jnp.mean(o * o, axis=-1, keepdims=True) + EPS) * g.astype(jnp.float32).reshape(h, 1, d)


def retention(q, k, v):
    b, h, s, d = q.shape
    n = s // CHUNK
    log_gamma = jnp.log1p(-(2.0 ** (-5.0 - jnp.arange(h, dtype=jnp.float32))))
    qc = q.reshape(b, h, n, CHUNK, d)
    kc = k.reshape(b, h, n, CHUNK, d) * (d ** -0.5)
    vc = v.reshape(b, h, n, CHUNK, d)
    pos = jnp.arange(CHUNK, dtype=jnp.float32)
    rel = jnp.abs(pos[:, None] - pos[None, :])
    d_intra = jnp.exp(log_gamma[:, None, None] * rel)
    scores = jnp.einsum('bhncd,bhnsd->bhncs', qc, kc) * d_intra[None, :, None]
    intra = jnp.einsum('bhncs,bhnse->bhnce', scores, vc)
    k_decay = jnp.exp(log_gamma[:, None] * (CHUNK - 1 - pos))
    kv = jnp.einsum('bhnsd,bhnse->bhnde', kc * k_decay[None, :, None, :, None], vc)
    chunk_decay = jnp.exp(log_gamma * CHUNK)[None, :, None, None]

    def step(state, kv_i):
        return chunk_decay * state + kv_i, state

    _, prev = lax.scan(step, jnp.zeros((b, h, d, d), jnp.float32), jnp.moveaxis(kv, 2, 0))
    prev = jnp.moveaxis(prev, 0, 2)
    q_decay = jnp.exp(log_gamma[:, None] * (pos + 1.0))
    inter = jnp.einsum('bhncd,bhnde->bhnce', qc * q_decay[None, :, None, :, None], prev)
    return (intra + inter).reshape(b, h, s, d)


def stick_breaking(q, k, v):
    b, h, s, d = q.shape
    scale = d ** -0.5
    outs = []
    for blk in range(s // SB_BLOCK):
        q0 = blk * SB_BLOCK
        end = q0 + SB_BLOCK
        z = jnp.einsum('bhtd,bhsd->bhts', q[:, :, q0:end], k[:, :, :end]) * scale
        t_idx = q0 + jnp.arange(SB_BLOCK)
        s_idx = jnp.arange(end)
        strict = s_idx[None, :] < t_idx[:, None]
        log_keep = jnp.where(strict, jax.nn.log_sigmoid(-z), 0.0)
        between = lax.cumsum(log_keep, axis=3, reverse=True) - log_keep
        a = jnp.where(strict, jnp.exp(jax.nn.log_sigmoid(z) + between), 0.0)
        outs.append(jnp.einsum('bhts,bhse->bhte', a, v[:, :, :end]))
    return jnp.concatenate(outs, axis=2)


def gla(q, k, v, log_alpha):
    b, h, s, d = q.shape
    n = s // CHUNK
    qc = q.reshape(b, h, n, CHUNK, d) * (d ** -0.5)
    kc = k.reshape(b, h, n, CHUNK, d)
    vc = v.reshape(b, h, n, CHUNK, d)
    bcum = jnp.cumsum(log_alpha.reshape(b, h, n, CHUNK, d), axis=3)
    blast = bcum[:, :, :, -1]
    kv = jnp.einsum('bhnsd,bhnse->bhnde', kc * jnp.exp(blast[:, :, :, None] - bcum), vc)

    def step(state, inp):
        q_i, k_i, v_i, b_i, kv_i, bl_i = inp
        inter = jnp.einsum('bhcd,bhde->bhce', q_i * jnp.exp(b_i), state)
        w = jnp.exp(-jnp.abs(b_i[:, :, :, None, :] - b_i[:, :, None, :, :]))
        scores = jnp.einsum('bhtd,bhsd,bhtsd->bhts', q_i, k_i, w)
        intra = jnp.einsum('bhts,bhse->bhte', scores, v_i)
        new_state = jnp.exp(bl_i)[..., None] * state + kv_i
        return new_state, inter + intra

    xs = tuple(jnp.moveaxis(t, 2, 0) for t in (qc, kc, vc, bcum, kv, blast))
    _, out = lax.scan(step, jnp.zeros((b, h, d, d), jnp.float32), xs)
    return jnp.moveaxis(out, 0, 2).reshape(b, h, s, d)


def token_mixer(hn, w_in, gla_w2, gla_b, ret_norm_g, gla_norm_g, w_out):
    widths = [RET_W] * 4 + [SB_W] * 3 + [GLA_W] * 4 + [GLA_GATE_RANK]
    idx = []
    acc = 0
    for wd in widths[:-1]:
        acc += wd
        idx.append(acc)
    proj = (hn @ w_in).astype(jnp.float32)
    rq, rk, rv, rg, sq, sk, sv, gq, gk, gv, gg, glr = jnp.split(proj, idx, axis=-1)
    ret = retention(rotary(split_heads(rq, RET_HEADS)), rotary(split_heads(rk, RET_HEADS)),
                    split_heads(rv, RET_HEADS))
    ret = merge_heads(head_groupnorm(ret, ret_norm_g)) * jax.nn.silu(rg)
    sb = merge_heads(stick_breaking(split_heads(sq, SB_HEADS), split_heads(sk, SB_HEADS),
                                    split_heads(sv, SB_HEADS)))
    log_alpha = jax.nn.log_sigmoid(glr @ gla_w2.astype(jnp.float32) + gla_b.astype(jnp.float32)) / GLA_TAU
    go = gla(split_heads(gq, GLA_HEADS), split_heads(gk, GLA_HEADS), split_heads(gv, GLA_HEADS),
             split_heads(log_alpha, GLA_HEADS))
    go = merge_heads(head_rmsnorm(go, gla_norm_g)) * jax.nn.silu(gg)
    mixed = jnp.concatenate([ret, sb, go], axis=-1).astype(hn.dtype)
    return mixed @ w_out


def conv_ffn(hn, w_up, conv_w, conv_b, w_down):
    s = hn.shape[1]
    up = hn @ w_up
    a, val = jnp.split(up, 2, axis=-1)
    ap = jnp.pad(a, ((0, 0), (FFN_CONV - 1, 0), (0, 0)))
    a = sum(ap[:, i:i + s] * conv_w[i] for i in range(FFN_CONV)) + conv_b
    return (jax.nn.gelu(a) * val) @ w_down


def setup_inputs(seed: int = 0) -> dict:
    key = jax.random.key(seed)
    ks = jax.random.split(key, 20)
    f32 = jnp.float32
    nrm = lambda k, shape, sc: jax.random.normal(k, shape, f32) * sc
    L, D = DEPTH, D_MODEL
    return {
        "x": nrm(ks[0], (BATCH, SEQ, D), 1.0),
        "c": nrm(ks[1], (BATCH, D), 1.0),
        "ada_w": nrm(ks[2], (L, D, 6 * D), 0.5 * D ** -0.5),
        "ada_b": nrm(ks[3], (L, 6 * D), 0.02),
        "pre_mix_g": 1.0 + nrm(ks[4], (L, D), 0.05),
        "post_mix_g": 1.0 + nrm(ks[5], (L, D), 0.05),
        "w_in": nrm(ks[6], (L, D, IN_COLS), D ** -0.5),
        "gla_w2": nrm(ks[7], (L, GLA_GATE_RANK, GLA_W), GLA_GATE_RANK ** -0.5),
        "gla_b": nrm(ks[8], (L, GLA_W), 0.1),
        "ret_norm_g": 1.0 + nrm(ks[9], (L, RET_W), 0.05),
        "gla_norm_g": 1.0 + nrm(ks[10], (L, GLA_W), 0.05),
        "w_out": nrm(ks[11], (L, D, D), D ** -0.5),
        "pre_ffn_g": 1.0 + nrm(ks[12], (L, D), 0.05),
        "post_ffn_g": 1.0 + nrm(ks[13], (L, D), 0.05),
        "w_up": nrm(ks[14], (L, D, 2 * D_FF), D ** -0.5),
        "conv_w": nrm(ks[15], (L, FFN_CONV, D_FF), FFN_CONV ** -0.5),
        "conv_b": nrm(ks[16], (L, D_FF), 0.02),
        "w_down": nrm(ks[17], (L, D_FF, D), D_FF ** -0.5),
    }


def reference(x, c, ada_w, ada_b, pre_mix_g, post_mix_g, w_in, gla_w2, gla_b, ret_norm_g,
              gla_norm_g, w_out, pre_ffn_g, post_ffn_g, w_up, conv_w, conv_b, w_down):
    for l in range(DEPTH):
        mod = jax.nn.silu(c) @ ada_w[l] + ada_b[l]
        sh1, sc1, g1, sh2, sc2, g2 = jnp.split(mod[:, None, :], 6, axis=-1)
        hn = rmsnorm(x, pre_mix_g[l]) * (1.0 + sc1) + sh1
        y = token_mixer(hn, w_in[l], gla_w2[l], gla_b[l], ret_norm_g[l], gla_norm_g[l], w_out[l])
        x = x + g1 * rmsnorm(y, post_mix_g[l])
        hn = rmsnorm(x, pre_ffn_g[l]) * (1.0 + sc2) + sh2
        y = conv_ffn(hn, w_up[l], conv_w[l], conv_b[l], w_down[l])
        x = x + g2 * rmsnorm(y, post_ffn_g[l])
    return x
```

```python
from contextlib import ExitStack, contextmanager
import numpy as np
import concourse.bass as bass
import concourse.mybir as mybir
from concourse.bass_utils import run_bass_kernel_spmd

F32 = mybir.dt.float32
F32R = mybir.dt.float32r
BF16 = mybir.dt.bfloat16
ALU = mybir.AluOpType
AF = mybir.ActivationFunctionType
AX = mybir.AxisListType

D = 1024
KC = 8
DFF = 2816
FC = DFF // 128
INC = 3728
EPS = 1e-6
ENGS = ("pe", "act", "dve", "pool", "sp")
import os
RETCUT = int(os.environ.get("RETCUT", "9"))
MERGE_MIX = int(os.environ.get("MERGE_MIX", "0"))


class Sem:
    def __init__(self, h):
        self.h = h
        self.cnt = 0


class Buf:
    def __init__(self, t, name):
        self.t = t
        self.name = name
        self.last_w = None
        self.readers = []
        self.dsem = None
        self.ssem = None
        self.ssem_h = None

    def __getitem__(self, idx):
        return self.t[idx]


class Prog:
    def __init__(self, nc, stack):
        self.nc = nc
        self.gstack = stack
        self.stack = stack
        self.q = {e: [] for e in ENGS}
        self.waited = {e: {} for e in ENGS}
        self.bufs = []
        self.nsem = 0
        self.sem = {e: self._newsem("s" + e) for e in ("pe", "act", "dve", "pool")}
        self.mksem = self._newsem("mk")
        self.pname = "ph"
        self.nested = 0
        self.dpool = []
        self.swused = set()
        self.swpool = []
        self.swmark = self.gstack.enter_context(self.nc.sbuf_tensor("swmark", [1, 8], F32))
        self.nbuf = 0

    def _newsem(self, name):
        self.nsem += 1
        return Sem(self.gstack.enter_context(self.nc.semaphore(f"{name}_{self.nsem}")))

    def buf(self, shape, dtype=F32, name=None, psum=False):
        self.nbuf += 1
        name = f"{name or 'b'}_{self.nbuf}"
        if psum:
            t = self.gstack.enter_context(self.nc.psum_tensor(name, list(shape), dtype))
        else:
            t = self.stack.enter_context(self.nc.sbuf_tensor(name, list(shape), dtype))
        b = Buf(t, name)
        self.bufs.append(b)
        return b

    def _wait(self, eng, dep):
        if dep is None:
            return
        if dep[0] == "multi":
            for d in dep[1]:
                self._wait(eng, d)
            return
        sem, val, src = dep
        if src == eng and eng == "pe":
            return
        w = self.waited[eng]
        if w.get(id(sem), 0) >= val:
            return
        w[id(sem)] = val
        self.q[eng].append(("wait", sem.h, val))

    def _deps(self, eng, reads, writes):
        for b in reads:
            self._wait(eng, b.last_w)
        for b in writes:
            self._wait(eng, b.last_w)
            for r in b.readers:
                self._wait(eng, r)

    @staticmethod
    def _mark(tok, reads, writes):
        for b in reads:
            b.readers.append(tok)
        for b in writes:
            b.last_w = tok
            b.readers = []

    def op(self, eng, fn, reads=(), writes=()):
        self._deps(eng, reads, writes)
        s = self.sem[eng]
        s.cnt += 1
        self.q[eng].append(("op", fn, s.h, 1))
        self._mark((s, s.cnt, eng), reads, writes)

    def pe(self, fn, reads=(), writes=()):
        self.op("pe", fn, reads, writes)

    def act(self, fn, reads=(), writes=()):
        self.op("act", fn, reads, writes)

    def dve(self, fn, reads=(), writes=()):
        self.op("dve", fn, reads, writes)

    def pool(self, fn, reads=(), writes=()):
        self.op("pool", fn, reads, writes)

    def _dma(self, eng, fn, b, is_load):
        if b.dsem is None:
            b.dsem = self.dpool.pop() if self.dpool else self._newsem("d")
        if is_load:
            self._deps(eng, (), (b,))
        else:
            self._deps(eng, (b,), ())
        s = b.dsem
        s.cnt += 16
        self.q[eng].append(("op", fn, s.h, 16))
        tok = (s, s.cnt, "dma")
        if is_load:
            self._mark(tok, (), (b,))
        else:
            self._mark(tok, (b,), ())

    def load(self, b, out_ap, in_ap, eng="sp", **kw):
        fn = lambda e: e.dma_start(out=out_ap, in_=in_ap, **kw)
        if eng != "pool":
            self._dma(eng, fn, b, True)
            return
        if b.ssem is None:
            b.ssem = self.swpool.pop() if self.swpool else self._newsem("sw")
        self._deps("pool", (), (b,))
        s = b.ssem
        s.cnt += 16
        self.q["pool"].append(("op", fn, s.h, 16))
        self._mark((s, s.cnt, "dma"), (), (b,))

    def store(self, b, out_ap, in_ap, eng="sp", **kw):
        self._dma(eng, lambda e: e.dma_start(out=out_ap, in_=in_ap, **kw), b, False)

    def flush(self):
        for b in self.bufs:
            if b.dsem is not None and b.dsem.cnt > 0:
                self._wait("sp", (b.dsem, b.dsem.cnt, "dma"))
            if b.ssem is not None:
                self._wait("sp", (b.ssem, b.ssem.cnt, "dma"))
        q = self.q

        def play(lst):
            def body(e):
                for it in lst:
                    if it[0] == "wait":
                        e.wait_ge(it[1], it[2])
                    elif it[0] == "clear":
                        e.sem_clear(it[1])
                    elif it[0] == "seminc":
                        e.sem_inc(it[1], 1)
                    else:
                        it[1](e).then_inc(it[2], it[3])
            return body

        with self.nc.named_scope(self.pname), self.nc.Block() as blk:
            if q["sp"]:
                blk.sync(play(q["sp"]))
            if q["pe"]:
                blk.tensor(play(q["pe"]))
            if q["act"]:
                blk.scalar(play(q["act"]))
            if q["dve"]:
                blk.vector(play(q["dve"]))
            if q["pool"]:
                blk.gpsimd(play(q["pool"]))
        self.q = {e: [] for e in ENGS}
        self.waited = {e: {} for e in ENGS}
        for b in self.bufs:
            b.last_w = None
            b.readers = []

    @contextmanager
    def phase(self, name="ph"):
        if self.nested:
            yield
            return
        self.pname = name
        n0 = len(self.bufs)
        with ExitStack() as ps:
            self.stack = ps
            yield
            self.flush()
            for b in self.bufs[n0:]:
                if b.dsem is not None:
                    self.dpool.append(b.dsem)
                    b.dsem = None
                if b.ssem is not None:
                    self.swpool.append(b.ssem)
                    b.ssem = None
            del self.bufs[n0:]
        self.stack = self.gstack


class Ring:
    def __init__(self, bufs):
        self.bufs = bufs
        self.i = -1

    def next(self):
        self.i = (self.i + 1) % len(self.bufs)
        return self.bufs[self.i]


def make_consts(S):
    c = {}
    c["ident"] = np.eye(128, dtype=np.float32)
    r = np.arange(128)
    perm = np.zeros((128, 128), np.float32)
    for rp in range(128):
        src = (rp // 64) * 64 + ((rp % 64) + 32) % 64
        perm[src, rp] = 1.0
    c["perm"] = perm
    half = 32
    pos = np.arange(S, dtype=np.float32)
    freqs = (np.float32(10000.0) ** (-(np.arange(half, dtype=np.float32) / np.float32(half)))).astype(np.float32)
    ang = (pos[:, None] * freqs[None, :]).astype(np.float32).astype(np.float64)
    fi = (r % 64) % 32
    c["cosT"] = np.cos(ang[:, fi]).T.astype(np.float32).copy()
    sgn = np.where((r % 64) < 32, -1.0, 1.0)
    c["sinT"] = (np.sin(ang[:, fi]).T * sgn[:, None]).astype(np.float32).copy()
    lg = np.log1p(-(2.0 ** (-5.0 - np.arange(6, dtype=np.float64))))
    s = np.arange(128)[:, None]
    t = np.arange(128)[None, :]
    rmask = np.zeros((128, 6, 128), np.float64)
    for h in range(6):
        same = (s // 64) == (t // 64)
        earlier = (s // 64) < (t // 64)
        m = np.where(same, np.exp(lg[h] * np.abs(t - s)), 0.0) + np.where(earlier, np.exp(lg[h] * (t - s)), 0.0)
        rmask[:, h, :] = m / 8.0
    c["rmask"] = rmask.astype(np.float32)
    kd = np.zeros((128, 384), np.float64)
    for h in range(6):
        kd[:, h * 64:(h + 1) * 64] = (np.exp(lg[h] * (127 - np.arange(128))) / 8.0)[:, None]
    c["kd"] = kd.astype(np.float32)
    qd = np.zeros((128, 3, 128), np.float64)
    g128 = np.zeros((128, 3), np.float64)
    for p in range(3):
        for rr in range(128):
            h = 2 * p + rr // 64
            qd[rr, p, :] = np.exp(lg[h] * (np.arange(128) + 1.0))
            g128[rr, p] = np.exp(lg[h] * 128.0)
    c["qd"] = qd.astype(np.float32)
    c["g128"] = g128.astype(np.float32)
    c["ntri"] = np.where(s >= t, -1.0, 0.0).astype(np.float32)
    c["nones"] = -np.ones((128, 128), np.float32)
    c["smask"] = np.where(s < t, 1.0, 0.0).astype(np.float32)
    c["m1"] = np.where(s <= t, 1.0, 0.0).astype(np.float32)
    c["m2"] = np.where((s > t) & ((s // 64) == (t // 64)), 1.0, 0.0).astype(np.float32)
    return c


CONST_SHAPES = lambda S: {
    "ident": [128, 128], "perm": [128, 128], "cosT": [128, S], "sinT": [128, S],
    "rmask": [128, 6, 128], "kd": [128, 384], "qd": [128, 3, 128], "g128": [128, 3],
    "ntri": [128, 128], "nones": [128, 128], "smask": [128, 128], "m1": [128, 128], "m2": [128, 128],
}

def pack_consts(consts, S):
    return np.ascontiguousarray(np.concatenate(
        [consts[k].reshape(128, -1) for k in CONST_SHAPES(S)], axis=1).astype(np.float32))


WEIGHT_SHAPES = lambda L: {
    "ada_w": [L, D, 6 * D], "ada_b": [L, 6 * D], "pre_mix_g": [L, D], "post_mix_g": [L, D],
    "w_in": [L, D, INC], "gla_w2": [L, 16, 256], "gla_b": [L, 256], "ret_norm_g": [L, 384],
    "gla_norm_g": [L, 256], "w_out": [L, D, D], "pre_ffn_g": [L, D], "post_ffn_g": [L, D],
    "w_up": [L, D, 2 * DFF], "conv_w": [L, 3, DFF], "conv_b": [L, DFF], "w_down": [L, DFF, D],
}


def build(S, L, stop_after=None, debug=False, only=None):
    NT = S // 128
    QG = min(512, S)
    NG = S // QG
    QB = QG // 128
    nc = bass.Bass("TRN2", target_bir_lowering=False)

    def din(name, shape):
        return nc.dram_tensor(name, list(shape), F32, kind="ExternalInput").ap()

    def dscr(name, shape):
        return nc.dram_tensor(name, list(shape), F32, kind="ExternalOutput" if debug else "Internal").ap()

    x_in = din("x", [S, D])
    c_in = din("c", [D])
    W = {k: din(k, v) for k, v in WEIGHT_SHAPES(L).items()}
    cshapes = CONST_SHAPES(S)
    ctot = sum(int(np.prod(v[1:])) for v in cshapes.values())
    cpack = din("kpack", [128, ctot])
    C = {}
    coff = 0
    for k, v in cshapes.items():
        n = int(np.prod(v[1:]))
        a = cpack[:, coff:coff + n]
        if len(v) == 3:
            a = a.rearrange("p (a b) -> p a b", a=v[1])
        C[k] = a
        coff += n
    out = nc.dram_tensor("out", [S, D], F32, kind="ExternalOutput").ap()

    modd = dscr("modd", [L, 6 * D])
    qrT = dscr("qrT", [3, 128, S]); krT = dscr("krT", [3, 128, S])
    sqT = dscr("sqT", [3, 128, S]); skT = dscr("skT", [3, 128, S])
    gqT = dscr("gqT", [2, 128, S]); gkT = dscr("gkT", [2, 128, S])
    glrT = dscr("glrT", [16, S])
    rvg = dscr("rvg", [S, 768]); svd = dscr("svd", [S, 384]); gvg = dscr("gvg", [S, 512])
    mixed = dscr("mixed", [S, D])
    sbT = dscr("sbT", [3, 128, S])
    x1d = dscr("x1d", [S, D])
    xbuf = dscr("xbuf", [S, D])

    with ExitStack() as gst:
        P = Prog(nc, gst)
        ps = [P.buf([128, 512], F32, f"ps{i}", psum=True) for i in range(8)]

        def bc_row(ap_row):
            return ap_row.partition_broadcast(128)

        def rstd_from_ss(ss, tmp, rs, cm05, n):
            P.dve(lambda e: e.tensor_scalar(tmp[:], ss[:], 1.0 / n, EPS, ALU.mult, ALU.add),
                  reads=[ss], writes=[tmp])
            P.pool(lambda e: e.tensor_tensor(rs[:], tmp[:], cm05[:, 0:rs.t.shape[1]], ALU.pow),
                   reads=[tmp, cm05], writes=[rs])

        def phase_mod(l):
            with P.phase("mod%d" % l):
                cT = P.buf([128, KC], F32, "cT")
                sT = P.buf([128, KC], F32, "sT")
                adab = P.buf([1, 6 * D], F32, "adab")
                modrow = P.buf([1, 6 * D], F32, "modrow")
                war = Ring([P.buf([128, KC, 512], F32, f"wa{i}") for i in range(2)])
                P.load(cT, cT[:], c_in.rearrange("(kc p) -> p kc", p=128), allow_slow_non_contiguous=True)
                P.load(adab, adab[:], W["ada_b"][l:l + 1, :])
                P.act(lambda e: e.activation(sT[:], cT[:], AF.Silu), reads=[cT], writes=[sT])
                awv = W["ada_w"][l].rearrange("(kc p) n -> p kc n", p=128)
                for cg in range(12):
                    wa = war.next()
                    P.load(wa, wa[:], awv[:, :, cg * 512:(cg + 1) * 512])
                    pz = ps[cg % 2]
                    for kc in range(KC):
                        P.pe(lambda e, pz=pz, wa=wa, kc=kc: e.matmul(
                            pz[0:1, :], sT[:, kc:kc + 1], wa[:, kc, :], start=(kc == 0), stop=(kc == KC - 1)),
                            reads=[sT, wa], writes=[pz])
                    P.dve(lambda e, pz=pz, cg=cg: e.tensor_tensor(
                        modrow[:, cg * 512:(cg + 1) * 512], pz[0:1, :], adab[:, cg * 512:(cg + 1) * 512], ALU.add),
                        reads=[pz, adab], writes=[modrow])
                P.store(modrow, modd[l:l + 1, :], modrow[:])

        def load_mod_tile(dst, l, idx):
            P.load(dst, dst[:], bc_row(modd[l, idx * D:(idx + 1) * D]))

        def norm_mod_transpose(xsrc, t, xt, hn, G, SH, xnT, col0, ident, st, cm05, junk):
            P.load(xt, xt[:], xsrc[t * 128:(t + 1) * 128, :])
            P.act(lambda e: e.activation(junk[:], xt[:], AF.Square, accum_out=st[:, 0:1]),
                  reads=[xt], writes=[junk, st])
            P.dve(lambda e: e.tensor_scalar(st[:, 1:2], st[:, 0:1], 1.0 / D, EPS, ALU.mult, ALU.add),
                  reads=[st], writes=[st])
            P.pool(lambda e: e.tensor_tensor(st[:, 2:3], st[:, 1:2], cm05[:, 0:1], ALU.pow),
                   reads=[st, cm05], writes=[st])
            P.dve(lambda e: e.scalar_tensor_tensor(hn[:], xt[:], st[:, 2:3], G[:], ALU.mult, ALU.mult),
                  reads=[xt, st, G], writes=[hn])
            P.dve(lambda e: e.tensor_tensor(hn[:], hn[:], SH[:], ALU.add), reads=[hn, SH], writes=[hn])
            for half in range(2):
                pt = ps[6 + half]
                for j in range(4):
                    kc = half * 4 + j
                    P.pe(lambda e, pt=pt, j=j, kc=kc: e.transpose(
                        pt[:, j * 128:(j + 1) * 128], hn[:, kc * 128:(kc + 1) * 128], ident[:]),
                        reads=[hn, ident], writes=[pt])
                P.act(lambda e, pt=pt, half=half: e.activation(
                    xnT[:, half * 4:(half + 1) * 4, col0:col0 + 128],
                    pt[:].rearrange("p (a b) -> p a b", a=4), AF.Copy), reads=[pt], writes=[xnT])

        def phase_proj(l, xsrc):
            with P.phase("proj%d" % l):
                ident = P.buf([128, 128], F32, "ident")
                perm = P.buf([128, 128], F32R, "perm")
                cosT = P.buf([128, S], F32, "cosT")
                sinT = P.buf([128, S], F32, "sinT")
                G1 = P.buf([128, D], F32, "G1")
                SH1 = P.buf([128, D], F32, "SH1")
                gtmp = P.buf([128, D], F32, "gtmp")
                cm05 = P.buf([128, 8], F32, "cm05")
                junk = P.buf([128, D], F32, "junk")
                xtr = Ring([P.buf([128, D], F32, f"xt{i}") for i in range(4)])
                hnr = Ring([P.buf([128, D], F32, f"hn{i}") for i in range(4)])
                str_ = Ring([P.buf([128, 4], F32, f"st{i}") for i in range(4)])
                xnTr = Ring([P.buf([128, KC, QG], F32R, f"xnT{i}") for i in range(2)])
                wfr = Ring([P.buf([128, KC, 512], F32R, f"wf{i}") for i in range(2)])
                wtr = Ring([P.buf([128, KC, 512], F32R, f"wt{i}") for i in range(2)])
                evr = Ring([P.buf([128, QG], F32R, f"ev{i}") for i in range(2)])
                t1 = P.buf([128, QG], F32, "t1")
                t2 = P.buf([128, QG], F32, "t2")
                resr = Ring([P.buf([128, QG], F32, f"res{i}") for i in range(2)])
                tokr = Ring([P.buf([128, 512], F32, f"tok{i}") for i in range(2)])

                P.load(ident, ident[:], C["ident"])
                P.load(perm, perm[:], C["perm"], eng="pool")
                P.load(cosT, cosT[:], C["cosT"])
                P.load(sinT, sinT[:], C["sinT"])
                P.pool(lambda e: e.memset(cm05[:], -0.5), writes=[cm05])
                P.load(G1, G1[:], bc_row(W["pre_mix_g"][l]))
                load_mod_tile(gtmp, l, 1)
                load_mod_tile(SH1, l, 0)
                P.dve(lambda e: e.scalar_tensor_tensor(G1[:], gtmp[:], 1.0, G1[:], ALU.add, ALU.mult),
                      reads=[gtmp, G1], writes=[G1])

                winv = W["w_in"][l].rearrange("(kc p) n -> p kc n", p=128)
                fm = []
                def sub(kind, dsts, base):
                    return [(128 * i, 128, kind, d, 0) for i, d in enumerate(dsts)]
                fm.append((0, 512, [(0, 128, "rot", qrT[0], 0), (128, 128, "rot", qrT[1], 0), (256, 128, "rot", qrT[2], 0),
                                    (384, 128, "rot", krT[0], 0)]))
                fm.append((512, 256, [(0, 128, "rot", krT[1], 0), (128, 128, "rot", krT[2], 0)]))
                fm.append((1536, 512, [(0, 128, "scale", sqT[0], 0), (128, 128, "scale", sqT[1], 0),
                                       (256, 128, "scale", sqT[2], 0), (384, 128, "plain", skT[0], 0)]))
                fm.append((2048, 256, [(0, 128, "plain", skT[1], 0), (128, 128, "plain", skT[2], 0)]))
                fm.append((2688, 512, [(0, 128, "plain", gqT[0], 0), (128, 128, "plain", gqT[1], 0),
                                       (256, 128, "plain", gkT[0], 0), (384, 128, "plain", gkT[1], 0)]))
                fm.append((3600, 128, [(0, 128, "glr", glrT, 0)]))
                tm = [(768, 384, rvg, 0), (1152, 384, rvg, 384), (2304, 384, svd, 0), (3200, 512, gvg, 0)]

                zi = 0
                def f1(g):
                    xn = xnTr.next()
                    for ti in range(QB):
                        norm_mod_transpose(xsrc, g * QB + ti, xtr.next(), hnr.next(), G1, SH1, xn, ti * 128,
                                           ident, str_.next(), cm05, junk)
                    return xn

                xn_next = f1(0)
                for g in range(NG):
                    gs = slice(g * QG, (g + 1) * QG)
                    xnT = xn_next
                    for (c0, ncol, subs) in fm:
                        wf = wfr.next()
                        P.load(wf, wf[:, :, 0:ncol], winv[:, :, c0:c0 + ncol], eng="pool")
                        for (co, m, kind, dst, _r0) in subs:
                            pz = ps[zi % 3]
                            zi += 1
                            for kc in range(KC):
                                P.pe(lambda e, pz=pz, wf=wf, kc=kc, m=m, co=co, xnT=xnT: e.matmul(
                                    pz[0:m, 0:QG], wf[:, kc, co:co + m], xnT[:, kc, :], start=(kc == 0), stop=(kc == KC - 1)),
                                    reads=[wf, xnT], writes=[pz])
                            ev = evr.next()
                            sc = 0.125 if kind == "scale" else 1.0
                            P.act(lambda e, pz=pz, ev=ev, m=m, sc=sc: e.activation(
                                ev[0:m, :], pz[0:m, 0:QG], AF.Copy, scale=sc), reads=[pz], writes=[ev])
                            if kind == "rot":
                                p2 = ps[3 + (zi % 2)]
                                P.pe(lambda e, p2=p2, ev=ev: e.matmul(p2[:, 0:QG], perm[:], ev[:], start=True, stop=True),
                                     reads=[perm, ev], writes=[p2])
                                P.dve(lambda e, ev=ev, gs=gs: e.tensor_tensor(t1[:], ev[:].bitcast(F32), cosT[:, gs], ALU.mult),
                                      reads=[ev, cosT], writes=[t1])
                                P.dve(lambda e, p2=p2, gs=gs: e.tensor_tensor(t2[:], p2[:, 0:QG], sinT[:, gs], ALU.mult),
                                      reads=[p2, sinT], writes=[t2])
                                res = resr.next()
                                P.dve(lambda e, res=res: e.tensor_tensor(res[:], t1[:], t2[:], ALU.add),
                                      reads=[t1, t2], writes=[res])
                                P.store(res, dst[:, gs], res[:])
                            elif kind == "glr":
                                P.store(ev, dst[:, gs], ev[112:128, :].bitcast(F32))
                            else:
                                P.store(ev, dst[0:m, gs], ev[0:m, :].bitcast(F32))
                    if g + 1 < NG:
                        xn_next = f1(g + 1)
                    for (c0, n, dst, dc0) in tm:
                        wt = wtr.next()
                        P.load(wt, wt[:, :, 0:n], winv[:, :, c0:c0 + n], eng="pool")
                        for ti in range(QB):
                            t = g * QB + ti
                            pz = ps[zi % 3]
                            zi += 1
                            for kc in range(KC):
                                P.pe(lambda e, pz=pz, wt=wt, kc=kc, n=n, ti=ti, xnT=xnT: e.matmul(
                                    pz[:, 0:n], xnT[:, kc, ti * 128:(ti + 1) * 128], wt[:, kc, 0:n],
                                    start=(kc == 0), stop=(kc == KC - 1)), reads=[wt, xnT], writes=[pz])
                            tk = tokr.next()
                            P.act(lambda e, pz=pz, tk=tk, n=n: e.activation(tk[:, 0:n], pz[:, 0:n], AF.Copy),
                                  reads=[pz], writes=[tk])
                            P.store(tk, dst[t * 128:(t + 1) * 128, dc0:dc0 + n], tk[:, 0:n])

        def phase_ret(l):
            with P.phase("ret%d" % l):
                ident = P.buf([128, 128], F32, "ident")
                rmask = P.buf([128, 6, 128], F32, "rmask")
                kd = P.buf([128, 384], F32, "kd")
                qd = P.buf([128, 3, 128], F32, "qd")
                g128 = P.buf([128, 3], F32, "g128")
                RG = P.buf([128, 384], F32, "RG")
                state = P.buf([128, 3, 64], F32, "rstate")
                cm05 = P.buf([128, 8], F32, "cm05")
                qz0r = Ring([P.buf([128, 3, 128], F32, f"qza{i}") for i in range(3)])
                qz1r = Ring([P.buf([128, 3, 128], F32, f"qzb{i}") for i in range(3)])
                ktr = Ring([P.buf([128, 3, 128], F32, f"kt{i}") for i in range(3)])
                vgr = Ring([P.buf([128, 768], F32, f"vg{i}") for i in range(3)])
                osb = P.buf([128, 384], F32, "osb")
                sq = P.buf([128, 384], F32, "sq")
                gate = P.buf([128, 384], F32, "gate")
                yr = Ring([P.buf([128, 384], F32, f"y{i}") for i in range(2)])
                s1 = P.buf([128, 6], F32, "s1")
                s2 = P.buf([128, 6], F32, "s2")
                mean = P.buf([128, 6], F32, "mean")
                msq = P.buf([128, 6], F32, "msq")
                var = P.buf([128, 6], F32, "var")
                rstd = P.buf([128, 6], F32, "rstd")
                psK, psS0, psS1, psO, psKV = ps[0], ps[1], ps[2], ps[3], ps[4]

                P.load(ident, ident[:], C["ident"])
                P.load(rmask, rmask[:], C["rmask"])
                P.load(kd, kd[:], C["kd"])
                P.load(qd, qd[:], C["qd"])
                P.load(g128, g128[:], C["g128"])
                P.load(RG, RG[:], bc_row(W["ret_norm_g"][l]))
                P.pool(lambda e: e.memset(cm05[:], -0.5), writes=[cm05])
                P.pool(lambda e: e.memset(state[:], 0.0), writes=[state])
                for b_ in qz0r.bufs:
                    P.pool(lambda e, b_=b_: e.memset(b_[64:128, :, :], 0.0), writes=[b_])
                for b_ in qz1r.bufs:
                    P.pool(lambda e, b_=b_: e.memset(b_[0:64, :, :], 0.0), writes=[b_])
                qv = qrT.rearrange("r p s -> p r s")
                kv = krT.rearrange("r p s -> p r s")
                def ret_loads(t):
                    ts_ = slice(t * 128, (t + 1) * 128)
                    qz0, qz1, kt, vg = qz0r.next(), qz1r.next(), ktr.next(), vgr.next()
                    P.load(qz0, qz0[0:64, :, :], qv[0:64, :, ts_])
                    P.load(qz1, qz1[64:128, :, :], qv[64:128, :, ts_])
                    P.load(kt, kt[:], kv[:, :, ts_])
                    P.load(vg, vg[:], rvg[ts_, :])
                    return qz0, qz1, kt, vg

                kdecr = Ring([P.buf([128, 384], F32, f"kdec{i}") for i in range(2)])
                qdeczr = Ring([[P.buf([128, 3, 128], F32, f"qdz{i}_{j}") for j in range(2)] for i in range(2)])
                scmr = Ring([P.buf([128, 768], F32, f"scm{i}") for i in range(2)])

                def stage_a(t, bufs):
                    qz0, qz1, kt, vg = bufs
                    qz = [qz0, qz1]
                    kdec, qdz, scm = kdecr.next(), qdeczr.next(), scmr.next()
                    for p in range(3):
                        P.pe(lambda e, p=p: e.transpose(psK[:, p * 128:(p + 1) * 128], kt[:, p, :], ident[:]),
                             reads=[kt, ident], writes=[psK])
                    P.dve(lambda e: e.tensor_tensor(kdec[:], psK[:, 0:384], kd[:], ALU.mult),
                          reads=[psK, kd], writes=[kdec])
                    for j in range(2):
                        P.dve(lambda e, j=j: e.tensor_tensor(qdz[j][:], qz[j][:], qd[:], ALU.mult),
                              reads=[qz[j], qd], writes=[qdz[j]])
                    for h in range(6):
                        p, j = h // 2, h % 2
                        pss = psS0 if h < 4 else psS1
                        P.pe(lambda e, pss=pss, h=h, p=p, j=j: e.matmul(
                            pss[:, (h % 4) * 128:(h % 4 + 1) * 128], kt[:, p, :], qz[j][:, p, :],
                            start=True, stop=True), reads=[kt, qz[j]], writes=[pss])
                    P.dve(lambda e: e.tensor_tensor(scm[:, 0:512], psS0[:], rmask[:, 0:4, :].rearrange("p a b -> p (a b)"),
                                                    ALU.mult), reads=[psS0, rmask], writes=[scm])
                    P.dve(lambda e: e.tensor_tensor(scm[:, 512:768], psS1[:, 0:256],
                                                    rmask[:, 4:6, :].rearrange("p a b -> p (a b)"), ALU.mult),
                          reads=[psS1, rmask], writes=[scm])
                    return kdec, qdz, scm

                def stage_b(t, bufs, aout):
                    qz0, qz1, kt, vg = bufs
                    kdec, qdz, scm = aout
                    ts_ = slice(t * 128, (t + 1) * 128)
                    for h in range(6):
                        p, j = h // 2, h % 2
                        P.pe(lambda e, h=h: e.matmul(
                            psO[:, h * 64:(h + 1) * 64], scm[:, h * 128:(h + 1) * 128], vg[:, h * 64:(h + 1) * 64],
                            start=True, stop=False), reads=[scm, vg], writes=[psO])
                        P.pe(lambda e, h=h, p=p, j=j: e.matmul(
                            psO[:, h * 64:(h + 1) * 64], qdz[j][:, p, :], state[:, p, :],
                            start=False, stop=True), reads=[qdz[j], state], writes=[psO])
                    for p in range(3):
                        P.pe(lambda e, p=p: e.matmul(
                            psKV[:, p * 128:(p + 1) * 128], kdec[:, p * 128:(p + 1) * 128], vg[:, p * 128:(p + 1) * 128],
                            start=True, stop=True), reads=[kdec, vg], writes=[psKV])
                    for h in range(6):
                        p, j = h // 2, h % 2
                        rows = slice(j * 64, (j + 1) * 64)
                        P.dve(lambda e, p=p, j=j, rows=rows: e.scalar_tensor_tensor(
                            state[rows, p, :], state[rows, p, :], g128[rows, p:p + 1],
                            psKV[rows, p * 128 + j * 64:p * 128 + (j + 1) * 64], ALU.mult, ALU.add),
                            reads=[state, g128, psKV], writes=[state])
                    P.act(lambda e: e.activation(osb[:], psO[:, 0:384], AF.Copy), reads=[psO], writes=[osb])
                    P.act(lambda e: e.activation(sq[:], osb[:], AF.Square), reads=[osb], writes=[sq])
                    P.act(lambda e: e.activation(gate[:], vg[:, 384:768], AF.Silu), reads=[vg], writes=[gate])
                    o3 = osb[:].rearrange("p (h d) -> p h d", h=6)
                    P.dve(lambda e: e.tensor_reduce(s1[:], o3, AX.X, ALU.add), reads=[osb], writes=[s1])
                    P.dve(lambda e: e.tensor_reduce(s2[:], sq[:].rearrange("p (h d) -> p h d", h=6), AX.X, ALU.add),
                          reads=[sq], writes=[s2])
                    P.dve(lambda e: e.tensor_scalar(mean[:], s1[:], 1.0 / 64, None, ALU.mult), reads=[s1], writes=[mean])
                    P.dve(lambda e: e.tensor_tensor(msq[:], mean[:], mean[:], ALU.mult), reads=[mean], writes=[msq])
                    P.dve(lambda e: e.scalar_tensor_tensor(var[:], s2[:], 1.0 / 64, msq[:], ALU.mult, ALU.subtract),
                          reads=[s2, msq], writes=[var])
                    P.dve(lambda e: e.tensor_scalar(var[:], var[:], EPS, None, ALU.add), reads=[var], writes=[var])
                    P.pool(lambda e: e.tensor_tensor(rstd[:], var[:], cm05[:, 0:6], ALU.pow),
                           reads=[var, cm05], writes=[rstd])
                    y = yr.next()
                    y3 = y[:].rearrange("p (h d) -> p h d", h=6)
                    P.dve(lambda e: e.tensor_tensor(y3, o3, mean[:].unsqueeze(2).broadcast_to([128, 6, 64]),
                                                    ALU.subtract), reads=[osb, mean], writes=[y])
                    P.dve(lambda e: e.tensor_tensor(y3, y3, rstd[:].unsqueeze(2).broadcast_to([128, 6, 64]),
                                                    ALU.mult), reads=[y, rstd], writes=[y])
                    P.pool(lambda e: e.tensor_tensor(y[:], y[:], RG[:], ALU.mult), reads=[y, RG], writes=[y])
                    P.pool(lambda e: e.tensor_tensor(y[:], y[:], gate[:], ALU.mult), reads=[y, gate], writes=[y])
                    P.store(y, mixed[ts_, 0:384], y[:])

                lb = {0: ret_loads(0)}
                if NT > 1:
                    lb[1] = ret_loads(1)
                ao = {0: stage_a(0, lb[0])}
                for t in range(NT):
                    if t + 2 < NT:
                        lb[t + 2] = ret_loads(t + 2)
                    if t + 1 < NT:
                        ao[t + 1] = stage_a(t + 1, lb[t + 1])
                    stage_b(t, lb[t], ao[t])
                    del lb[t], ao[t]
                    yield

        def phase_sb(l):
            with P.phase("sb%d" % l):
                ntri = P.buf([128, 128], F32R, "ntri")
                nones = P.buf([128, 128], F32R, "nones")
                zl = P.buf([128, 128], F32R, "zl")
                smask = P.buf([128, 128], F32, "smask")
                sv_all = P.buf([128, NT, 384], F32R, "sv_all")
                skr = Ring([P.buf([128, S], F32R, f"sk{i}") for i in range(2)])
                qg0r = Ring([P.buf([128, QG], F32R, f"qga{i}") for i in range(2)])
                qg1r = Ring([P.buf([128, QG], F32R, f"qgb{i}") for i in range(2)])
                Er = Ring([P.buf([128, QG], F32, f"E{i}") for i in range(3)])
                SPr = Ring([P.buf([128, QG], F32R, f"SP{i}") for i in range(4)])
                Ar = Ring([P.buf([128, QG], F32R, f"A{i}") for i in range(3)])
                RSr = Ring([P.buf([128, QG], F32R, f"RS{i}") for i in range(2)])
                ostr = Ring([P.buf([128, QG], F32, f"ost{i}") for i in range(2)])
                zr = Ring([ps[0], ps[1], ps[6]])
                ar = Ring([ps[2], ps[3]])
                por = Ring([ps[4], ps[5]])
                P.load(ntri, ntri[:], C["ntri"], eng="pool")
                P.load(nones, nones[:], C["nones"], eng="pool")
                P.load(smask, smask[:], C["smask"])
                P.pool(lambda e: e.memset(zl[:].bitcast(F32), 0.0), writes=[zl])
                for b_ in qg0r.bufs:
                    P.pool(lambda e, b_=b_: e.memset(b_[64:128, :].bitcast(F32), 0.0), writes=[b_])
                for b_ in qg1r.bufs:
                    P.pool(lambda e, b_=b_: e.memset(b_[0:64, :].bitcast(F32), 0.0), writes=[b_])
                P.load(sv_all, sv_all[:], svd.rearrange("(t p) c -> p t c", p=128), eng="pool")

                items = []
                for p in range(3):
                    for g in range(NG):
                        for j in range(2):
                            ctx = dict(p=p, g=g, j=j)
                            kbs = list(range(g * QB + QB - 1, -1, -1))
                            for i, kb in enumerate(kbs):
                                items.append((ctx, kb, i == 0, i == len(kbs) - 1))

                def ctx_begin(ctx):
                    p, g, j = ctx["p"], ctx["g"], ctx["j"]
                    if g == 0 and j == 0:
                        sk = skr.next()
                        P.load(sk, sk[:], skT[p], eng="pool")
                        ctx["sk_new"] = sk
                    if j == 0:
                        qga, qgb = qg0r.next(), qg1r.next()
                        P.load(qga, qga[0:64, :], sqT[p][0:64, g * QG:(g + 1) * QG], eng="pool")
                        P.load(qgb, qgb[64:128, :], sqT[p][64:128, g * QG:(g + 1) * QG], eng="pool")
                        cur["qga"], cur["qgb"] = qga, qgb
                        cur["ost"] = ostr.next()
                    if "sk_new" in ctx:
                        cur["sk"] = ctx["sk_new"]
                    ctx["sk"] = cur["sk"]
                    ctx["qg"] = cur["qga"] if j == 0 else cur["qgb"]
                    ctx["ost"] = cur["ost"]
                    ctx["RS"] = RSr.next()
                    ctx["po"] = por.next()
                    RS, po = ctx["RS"], ctx["po"]
                    P.pool(lambda e, RS=RS: e.memset(RS[:].bitcast(F32), 0.0), writes=[RS])
                    P.pe(lambda e, po=po: e.matmul(po[:, 0:QG], zl[:], sv_all[:, 0, 0:QG] if QG <= 384 else
                                                   sv_all[:, 0:2, :].rearrange("p a b -> p (a b)")[:, 0:QG],
                                                   start=True, stop=False), reads=[zl, sv_all], writes=[po])

                cur = {}

                def s1a(it):
                    ctx, kb, first, last = it
                    if first:
                        ctx_begin(ctx)
                    g = ctx["g"]
                    r = kb - g * QB
                    c0 = max(r, 0) * 128
                    cs = slice(c0, QG)
                    dg = slice(c0, c0 + 128)
                    pz, E = zr.next(), Er.next()
                    sk, qg = ctx["sk"], ctx["qg"]
                    P.pe(lambda e: e.matmul(pz[:, cs], sk[:, kb * 128:(kb + 1) * 128], qg[:, cs], start=True, stop=True),
                         reads=[sk, qg], writes=[pz])
                    P.act(lambda e: e.activation(E[:, cs], pz[:, cs], AF.Exp), reads=[pz], writes=[E])
                    return (E, r, cs, dg)

                def s1b(it, pre):
                    E, r, cs, dg = pre
                    SP = SPr.next()
                    P.act(lambda e: e.activation(SP[:, cs], E[:, cs], AF.Ln, bias=1.0), reads=[E], writes=[SP])
                    if r >= 0:
                        P.dve(lambda e: e.tensor_tensor(SP[:, dg], SP[:, dg].bitcast(F32), smask[:], ALU.mult),
                              reads=[SP, smask], writes=[SP])
                    return (SP, r, cs, dg)

                def rs_add(it, st):
                    ctx, kb, first, last = it
                    SP, r, cs, dg = st
                    RS = ctx["RS"]
                    P.dve(lambda e: e.tensor_tensor(RS[:, cs], RS[:, cs].bitcast(F32), SP[:, cs].bitcast(F32), ALU.add),
                          reads=[RS, SP], writes=[RS])

                def s2a(it, st):
                    ctx, kb, first, last = it
                    SP, r, cs, dg = st
                    sk, qg, RS = ctx["sk"], ctx["qg"], ctx["RS"]
                    pzb, A = ar.next(), Ar.next()
                    P.pe(lambda e: e.matmul(pzb[:, cs], sk[:, kb * 128:(kb + 1) * 128], qg[:, cs], start=True, stop=False),
                         reads=[sk, qg], writes=[pzb])
                    P.pe(lambda e: e.matmul(pzb[:, cs], ntri[:], SP[:, cs], start=False, stop=False),
                         reads=[ntri, SP], writes=[pzb])
                    P.pe(lambda e: e.matmul(pzb[:, cs], nones[:], RS[:, cs], start=False, stop=True),
                         reads=[nones, RS], writes=[pzb])
                    P.act(lambda e: e.activation(A[:, cs], pzb[:, cs], AF.Exp), reads=[pzb], writes=[A])
                    if r >= 0:
                        P.dve(lambda e: e.tensor_tensor(A[:, dg], A[:, dg].bitcast(F32), smask[:], ALU.mult),
                              reads=[A, smask], writes=[A])
                    return A

                def s2b(it, st, A):
                    ctx, kb, first, last = it
                    SP, r, cs, dg = st
                    p, j = ctx["p"], ctx["j"]
                    po = ctx["po"]
                    P.pe(lambda e: e.matmul(po[:, cs], sv_all[:, kb, p * 128:(p + 1) * 128], A[:, cs],
                                            start=False, stop=False), reads=[sv_all, A], writes=[po])
                    if last:
                        P.pe(lambda e: e.matmul(po[:, 0:QG], zl[:], sv_all[:, 0, 0:QG] if QG <= 384 else
                                                sv_all[:, 0:2, :].rearrange("p a b -> p (a b)")[:, 0:QG],
                                                start=False, stop=True), reads=[zl, sv_all], writes=[po])
                        ost = ctx["ost"]
                        rows = slice(j * 64, (j + 1) * 64)
                        P.act(lambda e: e.activation(ost[rows, :], po[rows, 0:QG], AF.Copy), reads=[po], writes=[ost])
                        if j == 1:
                            g = ctx["g"]
                            P.store(ost, sbT[p][:, g * QG:(g + 1) * QG], ost[:])

                n_it = len(items)
                pres, sts, As = {}, {}, {}
                pres[0] = s1a(items[0])
                if n_it > 1:
                    pres[1] = s1a(items[1])
                sts[0] = s1b(items[0], pres.pop(0))
                for i in range(n_it + 1):
                    if 1 <= i < n_it and not items[i][2]:
                        rs_add(items[i - 1], sts[i - 1])
                    if i + 2 < n_it:
                        pres[i + 2] = s1a(items[i + 2])
                    if i + 1 < n_it:
                        sts[i + 1] = s1b(items[i + 1], pres.pop(i + 1))
                    if i < n_it:
                        As[i] = s2a(items[i], sts[i])
                    if i >= 1:
                        s2b(items[i - 1], sts[i - 1], As[i - 1])
                        del sts[i - 1], As[i - 1]

        def phase_gla(l):
            with P.phase("gla%d" % l):
                ident = P.buf([128, 128], F32, "ident")
                m1 = P.buf([128, 128], F32, "m1")
                m2 = P.buf([128, 128], F32, "m2")
                ones = P.buf([128, 128], F32, "ones")
                w2 = P.buf([128, 256], F32, "w2")
                gb = P.buf([128, 2], F32, "gb")
                nb = P.buf([128, 2], F32, "nb")
                GG = P.buf([128, 256], F32, "GG")
                state = P.buf([128, 2, 64], F32, "gstate")
                cm05 = P.buf([128, 8], F32, "cm05")
                glr = P.buf([128, QG], F32, "glr")
                gq0r = Ring([P.buf([128, QG], F32, f"gqa{i}") for i in range(2)])
                gq1r = Ring([P.buf([128, QG], F32, f"gqb{i}") for i in range(2)])
                gkr = Ring([P.buf([128, QG], F32, f"gk{i}") for i in range(2)])
                e_ = P.buf([128, QG], F32, "e")
                sp = P.buf([128, QG], F32, "sp")
                bsp = P.buf([128, QG], F32, "bsp")
                EM = P.buf([128, QG], F32, "EM")
                EK = P.buf([128, QG], F32, "EK")
                nbl = P.buf([128, QB], F32, "nbl")
                vgr = Ring([P.buf([128, 512], F32, f"vg{i}") for i in range(3)])
                tmp2 = P.buf([128, 128], F32, "tmp2")
                osb = P.buf([128, 128], F32, "osb")
                sq = P.buf([128, 128], F32, "sq")
                gate = P.buf([128, 128], F32, "gate")
                s2 = P.buf([128, 2], F32, "s2")
                rstd = P.buf([128, 2], F32, "rstd")
                yr = Ring([P.buf([128, 128], F32, f"y{i}") for i in range(2)])
                psZ, psS, psT, psO, psKV = ps[5], ps[6], ps[7], ps[5], ps[7]

                P.load(ident, ident[:], C["ident"])
                P.load(m1, m1[:], C["m1"])
                P.load(m2, m2[:], C["m2"])
                P.pool(lambda e: e.memset(w2[:], 0.0), writes=[w2])
                P.pool(lambda e: e.memset(glr[:], 0.0), writes=[glr])
                P.load(w2, w2[0:16, :], W["gla_w2"][l])
                P.load(gb, gb[:], W["gla_b"][l].rearrange("(g p) -> p g", p=128), allow_slow_non_contiguous=True)
                P.load(GG, GG[:], bc_row(W["gla_norm_g"][l]))
                P.pool(lambda e: e.memset(cm05[:], -0.5), writes=[cm05])
                P.pool(lambda e: e.memset(ones[:], 1.0), writes=[ones])
                P.pool(lambda e: e.memset(state[:], 0.0), writes=[state])
                P.dve(lambda e: e.tensor_scalar(nb[:], gb[:], -1.0, None, ALU.mult), reads=[gb], writes=[nb])
                for b_ in gq0r.bufs:
                    P.pool(lambda e, b_=b_: e.memset(b_[64:128, :], 0.0), writes=[b_])
                for b_ in gq1r.bufs:
                    P.pool(lambda e, b_=b_: e.memset(b_[0:64, :], 0.0), writes=[b_])
                d1r = Ring([dict(qpz=[P.buf([128, QG], F32, f"qpz{i}_{j}") for j in range(2)],
                                 qmz=[P.buf([128, QG], F32, f"qmz{i}_{j}") for j in range(2)],
                                 km=P.buf([128, QG], F32, f"km{i}"), kp=P.buf([128, QG], F32, f"kp{i}"),
                                 ks=P.buf([128, QG], F32, f"ks{i}"), EP=P.buf([128, QG], F32, f"EP{i}"))
                            for i in range(2)])
                scr = Ring([P.buf([128, 2, 128], F32, f"sc{i}") for i in range(2)])
                kstr = Ring([P.buf([128, 128], F32, f"kst{i}") for i in range(2)])
                glr_loaded = {}

                def d1(g, cg):
                    gs = slice(g * QG, (g + 1) * QG)
                    if g not in glr_loaded:
                        P.load(glr, glr[0:16, :], glrT[:, gs])
                        glr_loaded[g] = True
                    o = d1r.next()
                    qpz, qmz, km, kp, ks, EP = o["qpz"], o["qmz"], o["km"], o["kp"], o["ks"], o["EP"]
                    gqa, gqb, gk = gq0r.next(), gq1r.next(), gkr.next()
                    gqz = [gqa, gqb]
                    P.load(gqa, gqa[0:64, :], gqT[cg][0:64, gs])
                    P.load(gqb, gqb[64:128, :], gqT[cg][64:128, gs])
                    P.load(gk, gk[:], gkT[cg][:, gs])
                    P.pe(lambda e: e.matmul(psZ[:, 0:QG], w2[:, cg * 128:(cg + 1) * 128], glr[:],
                                            start=True, stop=True), reads=[w2, glr], writes=[psZ])
                    P.act(lambda e: e.activation(e_[:], psZ[:, 0:QG], AF.Exp, bias=nb[:, cg:cg + 1], scale=-1.0),
                          reads=[psZ, nb], writes=[e_])
                    P.act(lambda e: e.activation(sp[:], e_[:], AF.Ln, bias=1.0), reads=[e_], writes=[sp])
                    for ti in range(QB):
                        sl = slice(ti * 128, (ti + 1) * 128)
                        P.dve(lambda e, sl=sl: e.tensor_tensor_scan(bsp[:, sl], ones[:], sp[:, sl], 0.0, ALU.mult, ALU.add),
                              reads=[ones, sp], writes=[bsp])
                    P.act(lambda e: e.activation(EP[:], bsp[:], AF.Exp, scale=-1.0 / 16), reads=[bsp], writes=[EP])
                    P.act(lambda e: e.activation(EM[:], bsp[:], AF.Exp, scale=1.0 / 16), reads=[bsp], writes=[EM])
                    P.dve(lambda e: e.tensor_scalar(
                        nbl[:], bsp[:].rearrange("p (t s) -> p t s", s=128)[:, :, 127], -1.0 / 16, None, ALU.mult),
                        reads=[bsp], writes=[nbl])
                    for ti in range(QB):
                        sl = slice(ti * 128, (ti + 1) * 128)
                        P.act(lambda e, sl=sl, ti=ti: e.activation(EK[:, sl], bsp[:, sl], AF.Exp,
                                                                   bias=nbl[:, ti:ti + 1], scale=1.0 / 16),
                              reads=[bsp, nbl], writes=[EK])
                    for j in range(2):
                        P.dve(lambda e, j=j: e.scalar_tensor_tensor(
                            qpz[j][:], gqz[j][:], 0.125, EP[:], ALU.mult, ALU.mult),
                            reads=[gqz[j], EP], writes=[qpz[j]])
                        P.dve(lambda e, j=j: e.scalar_tensor_tensor(
                            qmz[j][:], gqz[j][:], 0.125, EM[:], ALU.mult, ALU.mult),
                            reads=[gqz[j], EM], writes=[qmz[j]])
                    P.dve(lambda e: e.tensor_tensor(km[:], gk[:], EM[:], ALU.mult), reads=[gk, EM], writes=[km])
                    P.dve(lambda e: e.tensor_tensor(kp[:], gk[:], EP[:], ALU.mult), reads=[gk, EP], writes=[kp])
                    P.dve(lambda e: e.tensor_tensor(ks[:], gk[:], EK[:], ALU.mult), reads=[gk, EK], writes=[ks])
                    return o

                def gla_vg(g, ti):
                    t = g * QB + ti
                    vg = vgr.next()
                    P.load(vg, vg[:], gvg[t * 128:(t + 1) * 128, :])
                    return vg

                def stage_a(o, ti):
                    qpz, qmz, km, kp, ks = o["qpz"], o["qmz"], o["km"], o["kp"], o["ks"]
                    sl = slice(ti * 128, (ti + 1) * 128)
                    sc, kst = scr.next(), kstr.next()
                    for j in range(2):
                        P.pe(lambda e, j=j: e.matmul(
                            psS[:, j * 256:j * 256 + 128], km[:, sl], qpz[j][:, sl], start=True, stop=True),
                            reads=[km, qpz[j]], writes=[psS])
                        P.pe(lambda e, j=j: e.matmul(
                            psS[:, j * 256 + 128:j * 256 + 256], kp[:, sl], qmz[j][:, sl], start=True, stop=True),
                            reads=[kp, qmz[j]], writes=[psS])
                    for j in range(2):
                        P.dve(lambda e, j=j: e.tensor_tensor(sc[:, j, :], psS[:, j * 256:j * 256 + 128], m1[:], ALU.mult),
                              reads=[psS, m1], writes=[sc])
                        P.dve(lambda e, j=j: e.tensor_tensor(tmp2[:], psS[:, j * 256 + 128:j * 256 + 256], m2[:], ALU.mult),
                              reads=[psS, m2], writes=[tmp2])
                        P.dve(lambda e, j=j: e.tensor_tensor(sc[:, j, :], sc[:, j, :], tmp2[:], ALU.add),
                              reads=[sc, tmp2], writes=[sc])
                    P.pe(lambda e: e.transpose(psT[:, 0:128], ks[:, sl], ident[:]), reads=[ks, ident], writes=[psT])
                    P.act(lambda e: e.activation(kst[:], psT[:, 0:128], AF.Copy), reads=[psT], writes=[kst])
                    return sc, kst

                def stage_b(o, g, cg, ti, vg, aout):
                    qpz, EP = o["qpz"], o["EP"]
                    sc, kst = aout
                    t = g * QB + ti
                    sl = slice(ti * 128, (ti + 1) * 128)
                    ts_ = slice(t * 128, (t + 1) * 128)
                    for j in range(2):
                        h = 2 * cg + j
                        P.pe(lambda e, j=j, h=h: e.matmul(
                            psO[:, j * 64:(j + 1) * 64], sc[:, j, :], vg[:, h * 64:(h + 1) * 64],
                            start=True, stop=False), reads=[sc, vg], writes=[psO])
                        P.pe(lambda e, j=j: e.matmul(
                            psO[:, j * 64:(j + 1) * 64], qpz[j][:, sl], state[:, cg, :],
                            start=False, stop=True), reads=[qpz[j], state], writes=[psO])
                    P.pe(lambda e: e.matmul(psKV[:, 0:128], kst[:], vg[:, cg * 128:(cg + 1) * 128],
                                            start=True, stop=True), reads=[kst, vg], writes=[psKV])
                    for j in range(2):
                        rows = slice(j * 64, (j + 1) * 64)
                        lc = ti * 128 + 127
                        P.dve(lambda e, j=j, rows=rows, lc=lc: e.scalar_tensor_tensor(
                            state[rows, cg, :], state[rows, cg, :], EP[rows, lc:lc + 1],
                            psKV[rows, j * 64:(j + 1) * 64], ALU.mult, ALU.add),
                            reads=[state, EP, psKV], writes=[state])
                    P.act(lambda e: e.activation(osb[:], psO[:, 0:128], AF.Copy), reads=[psO], writes=[osb])
                    P.act(lambda e: e.activation(sq[:], osb[:], AF.Square), reads=[osb], writes=[sq])
                    P.act(lambda e: e.activation(gate[:], vg[:, 256 + cg * 128:256 + (cg + 1) * 128], AF.Silu),
                          reads=[vg], writes=[gate])
                    P.dve(lambda e: e.tensor_reduce(s2[:], sq[:].rearrange("p (h d) -> p h d", h=2), AX.X, ALU.add),
                          reads=[sq], writes=[s2])
                    P.dve(lambda e: e.tensor_scalar(s2[:], s2[:], 1.0 / 64, EPS, ALU.mult, ALU.add),
                          reads=[s2], writes=[s2])
                    P.pool(lambda e: e.tensor_tensor(rstd[:], s2[:], cm05[:, 0:2], ALU.pow),
                           reads=[s2, cm05], writes=[rstd])
                    y = yr.next()
                    P.dve(lambda e: e.tensor_tensor(
                        y[:].rearrange("p (h d) -> p h d", h=2), osb[:].rearrange("p (h d) -> p h d", h=2),
                        rstd[:].unsqueeze(2).broadcast_to([128, 2, 64]), ALU.mult),
                        reads=[osb, rstd], writes=[y])
                    P.pool(lambda e: e.tensor_tensor(y[:], y[:], GG[:, cg * 128:(cg + 1) * 128], ALU.mult),
                           reads=[y, GG], writes=[y])
                    P.pool(lambda e: e.tensor_tensor(y[:], y[:], gate[:], ALU.mult), reads=[y, gate], writes=[y])
                    P.store(y, mixed[ts_, 768 + cg * 128:768 + (cg + 1) * 128], y[:])

                seq = [(g, cg, ti) for g in range(NG) for cg in range(2) for ti in range(QB)]
                ctxs = {}

                def ctx_of(g, cg):
                    if (g, cg) not in ctxs:
                        ctxs[(g, cg)] = d1(g, cg)
                    return ctxs[(g, cg)]

                n = len(seq)
                vgs = {k: gla_vg(seq[k][0], seq[k][2]) for k in range(min(2, n))}
                aos = {0: stage_a(ctx_of(seq[0][0], seq[0][1]), seq[0][2])}
                for k in range(n):
                    g, cg, ti = seq[k]
                    if k + 2 < n:
                        vgs[k + 2] = gla_vg(seq[k + 2][0], seq[k + 2][2])
                    if k + 1 < n:
                        g1, cg1, ti1 = seq[k + 1]
                        aos[k + 1] = stage_a(ctx_of(g1, cg1), ti1)
                    stage_b(ctx_of(g, cg), g, cg, ti, vgs[k], aos[k])
                    del vgs[k], aos[k]
                    yield

        def post_norm_residual(pys, xt, GP, o, st, cm05, junk, dst_rows):
            for h in range(2):
                P.act(lambda e, h=h: e.activation(junk[:, 0:512], pys[h][1], AF.Square, accum_out=st[:, h:h + 1]),
                      reads=[pys[h][0]], writes=[junk, st])
            P.dve(lambda e: e.tensor_tensor(st[:, 2:3], st[:, 0:1], st[:, 1:2], ALU.add), reads=[st], writes=[st])
            P.dve(lambda e: e.tensor_scalar(st[:, 2:3], st[:, 2:3], 1.0 / D, EPS, ALU.mult, ALU.add),
                  reads=[st], writes=[st])
            P.pool(lambda e: e.tensor_tensor(st[:, 3:4], st[:, 2:3], cm05[:, 0:1], ALU.pow),
                   reads=[st, cm05], writes=[st])
            for h in range(2):
                P.dve(lambda e, h=h: e.scalar_tensor_tensor(
                    o[:, h * 512:(h + 1) * 512], pys[h][1], st[:, 3:4], GP[:, h * 512:(h + 1) * 512],
                    ALU.mult, ALU.mult), reads=[pys[h][0], st, GP], writes=[o])
            P.dve(lambda e: e.tensor_tensor(o[:], o[:], xt[:], ALU.add), reads=[o, xt], writes=[o])
            P.store(o, dst_rows, o[:])

        def phase_out(l, xsrc):
            with P.phase("out%d" % l):
                ident = P.buf([128, 128], F32, "ident")
                wsb = P.buf([128, KC, D], F32R, "wout")
                GP = P.buf([128, D], F32, "GP1")
                gtmp = P.buf([128, D], F32, "gtmp")
                cm05 = P.buf([128, 8], F32, "cm05")
                junk = P.buf([128, 512], F32, "junk")
                mxr = Ring([P.buf([128, D], F32, f"mx{i}") for i in range(3)])
                xtr = Ring([P.buf([128, D], F32, f"xt{i}") for i in range(3)])
                mTr = Ring([P.buf([128, KC, 128], F32R, f"mT{i}") for i in range(3)])
                otr = Ring([P.buf([128, D], F32, f"ot{i}") for i in range(2)])
                str_ = Ring([P.buf([128, 4], F32, f"st{i}") for i in range(2)])
                P.load(ident, ident[:], C["ident"])
                P.load(wsb, wsb[:], W["w_out"][l].rearrange("(kc p) n -> p kc n", p=128), eng="pool")
                P.load(GP, GP[:], bc_row(W["post_mix_g"][l]))
                load_mod_tile(gtmp, l, 2)
                P.dve(lambda e: e.tensor_tensor(GP[:], GP[:], gtmp[:], ALU.mult), reads=[GP, gtmp], writes=[GP])
                P.pool(lambda e: e.memset(cm05[:], -0.5), writes=[cm05])
                def out_loads(t):
                    ts_ = slice(t * 128, (t + 1) * 128)
                    m, xx, mt = mxr.next(), xtr.next(), mTr.next()
                    P.load(m, m[:, 0:384], mixed[ts_, 0:384])
                    P.load(m, m[:, 768:1024], mixed[ts_, 768:1024])
                    P.load(xx, xx[:], xsrc[ts_, :])
                    P.load(mt, mt[:, 3:6, :], sbT.rearrange("r p s -> p r s")[:, :, ts_], eng="pool")
                    return m, xx, mt

                def out_a(t, bufs):
                    m, xx, mt = bufs
                    pt = ps[6 + (t % 2)]
                    for j, kc in enumerate((0, 1, 2)):
                        P.pe(lambda e, j=j, kc=kc: e.transpose(
                            pt[:, j * 128:(j + 1) * 128], m[:, kc * 128:(kc + 1) * 128], ident[:]),
                            reads=[m, ident], writes=[pt])
                    P.act(lambda e: e.activation(
                        mt[:, 0:3, :], pt[:, 0:384].rearrange("p (a b) -> p a b", a=3), AF.Copy),
                        reads=[pt], writes=[mt])
                    pt2 = ps[4 + (t % 2)]
                    for j, kc in enumerate((6, 7)):
                        P.pe(lambda e, j=j, kc=kc: e.transpose(
                            pt2[:, j * 128:(j + 1) * 128], m[:, kc * 128:(kc + 1) * 128], ident[:]),
                            reads=[m, ident], writes=[pt2])
                    P.act(lambda e: e.activation(
                        mt[:, 6:8, :], pt2[:, 0:256].rearrange("p (a b) -> p a b", a=2), AF.Copy),
                        reads=[pt2], writes=[mt])
                    pys = [ps[(t % 2) * 2], ps[(t % 2) * 2 + 1]]
                    for h in range(2):
                        for kc in range(KC):
                            P.pe(lambda e, py=pys[h], kc=kc, h=h: e.matmul(
                                py[:], mt[:, kc, :], wsb[:, kc, h * 512:(h + 1) * 512],
                                start=(kc == 0), stop=(kc == KC - 1)), reads=[mt, wsb], writes=[pys[h]])
                    return pys

                lb = {0: out_loads(0)}
                if NT > 1:
                    lb[1] = out_loads(1)
                pyo = {0: out_a(0, lb[0])}
                for t in range(NT):
                    ts_ = slice(t * 128, (t + 1) * 128)
                    if t + 2 < NT:
                        lb[t + 2] = out_loads(t + 2)
                    if t + 1 < NT:
                        pyo[t + 1] = out_a(t + 1, lb[t + 1])
                    post_norm_residual([(b_, b_[:]) for b_ in pyo[t]], lb[t][1], GP, otr.next(), str_.next(),
                                       cm05, junk, x1d[ts_, :])
                    del lb[t], pyo[t]

        def phase_ffn(l, dst):
            with P.phase("ffn%d" % l):
                ident = P.buf([128, 128], F32, "ident")
                G2 = P.buf([128, D], F32, "G2")
                SH2 = P.buf([128, D], F32, "SH2")
                GP2 = P.buf([128, D], F32, "GP2")
                cm05 = P.buf([128, 8], F32, "cm05")
                junk = P.buf([128, D], F32, "junk")
                gtmp = junk
                cw = P.buf([128, 3, FC], F32, "cw")
                cb = P.buf([128, FC], F32, "cb")
                halo = P.buf([128, FC, 2], F32, "halo")
                xtr = Ring([P.buf([128, D], F32, f"xt{i}") for i in range(2 * QB)])
                hnr = Ring([P.buf([128, D], F32, f"hn{i}") for i in range(2)])
                str_ = Ring([P.buf([128, 4], F32, f"st{i}") for i in range(4)])
                st2r = Ring([P.buf([128, 4], F32, f"su{i}") for i in range(2)])
                xnTr = Ring([P.buf([128, KC, QG], BF16, f"xn2T{i}") for i in range(2)])
                hT = P.buf([128, FC, QG], BF16, "hT")
                NSC = 4
                wavr = Ring([P.buf([128, KC, 2, NSC * 128], BF16, f"wav{i}") for i in range(2)])
                wdr = Ring([P.buf([128, FC, 512], BF16, f"wd{i}") for i in range(2)])
                ASr = Ring([P.buf([128, QG + 2], F32, f"AS{i}") for i in range(2)])
                cvr = Ring([P.buf([128, QG], F32, f"cv{i}") for i in range(2)])
                glr_ = Ring([P.buf([128, QG], F32, f"gl{i}") for i in range(2)])
                otr = Ring([P.buf([128, D], F32, f"ot{i}") for i in range(2)])
                y2s = [P.buf([128, D], F32, f"y2s{i}") for i in range(QB)]
                P.load(ident, ident[:], C["ident"])
                P.load(G2, G2[:], bc_row(W["pre_ffn_g"][l]))
                load_mod_tile(gtmp, l, 4)
                load_mod_tile(SH2, l, 3)
                P.dve(lambda e: e.scalar_tensor_tensor(G2[:], gtmp[:], 1.0, G2[:], ALU.add, ALU.mult),
                      reads=[gtmp, G2], writes=[G2])
                P.load(GP2, GP2[:], bc_row(W["post_ffn_g"][l]))
                load_mod_tile(gtmp, l, 5)
                P.dve(lambda e: e.tensor_tensor(GP2[:], GP2[:], gtmp[:], ALU.mult), reads=[GP2, gtmp], writes=[GP2])
                P.pool(lambda e: e.memset(cm05[:], -0.5), writes=[cm05])
                P.pool(lambda e: e.memset(halo[:], 0.0), writes=[halo])
                P.load(cw, cw[:], W["conv_w"][l].rearrange("i (fc p) -> p i fc", p=128), allow_slow_non_contiguous=True)
                P.load(cb, cb[:], W["conv_b"][l].rearrange("(fc p) -> p fc", p=128), allow_slow_non_contiguous=True)
                wuv = W["w_up"][l].rearrange("(kc p) n -> p kc n", p=128)
                wdv = W["w_down"][l].rearrange("(fc p) n -> p fc n", p=128)
                zi = 0

                def f1(g):
                    xn = xnTr.next()
                    xts = []
                    for ti in range(QB):
                        xt = xtr.next()
                        xts.append(xt)
                        norm_mod_transpose(x1d, g * QB + ti, xt, hnr.next(), G2, SH2, xn, ti * 128,
                                           ident, str_.next(), cm05, junk)
                    return xn, xts

                nxt = f1(0)
                for g in range(NG):
                    xnT, xts = nxt
                    for sc in range(0, FC, NSC):
                        nsc = min(NSC, FC - sc)
                        wav = wavr.next()
                        P.load(wav, wav[:, :, 0, 0:nsc * 128], wuv[:, :, sc * 128:(sc + nsc) * 128], eng="pool")
                        P.load(wav, wav[:, :, 1, 0:nsc * 128], wuv[:, :, DFF + sc * 128:DFF + (sc + nsc) * 128], eng="pool")
                        for fi in range(nsc):
                            fc = sc + fi
                            pa = ps[zi % 2]
                            pv = ps[2 + zi % 2]
                            zi += 1
                            for kc in range(KC):
                                P.pe(lambda e, pa=pa, wav=wav, kc=kc, fi=fi, xnT=xnT: e.matmul(
                                    pa[:, 0:QG], wav[:, kc, 0, fi * 128:(fi + 1) * 128], xnT[:, kc, :],
                                    start=(kc == 0), stop=(kc == KC - 1)), reads=[wav, xnT], writes=[pa])
                            for kc in range(KC):
                                P.pe(lambda e, pv=pv, wav=wav, kc=kc, fi=fi, xnT=xnT: e.matmul(
                                    pv[:, 0:QG], wav[:, kc, 1, fi * 128:(fi + 1) * 128], xnT[:, kc, :],
                                    start=(kc == 0), stop=(kc == KC - 1)), reads=[wav, xnT], writes=[pv])
                            AS, cv, gl = ASr.next(), cvr.next(), glr_.next()
                            P.act(lambda e, AS=AS, pa=pa: e.activation(AS[:, 2:QG + 2], pa[:, 0:QG], AF.Copy),
                                  reads=[pa], writes=[AS])
                            P.dve(lambda e, AS=AS, fc=fc: e.tensor_copy(AS[:, 0:2], halo[:, fc, :]),
                                  reads=[halo, AS], writes=[AS])
                            P.dve(lambda e, AS=AS, fc=fc: e.tensor_copy(halo[:, fc, :], AS[:, QG:QG + 2]),
                                  reads=[AS], writes=[halo])
                            P.dve(lambda e, AS=AS, fc=fc, cv=cv: e.tensor_scalar(
                                cv[:], AS[:, 2:QG + 2], cw[:, 2, fc:fc + 1], cb[:, fc:fc + 1], ALU.mult, ALU.add),
                                reads=[AS, cw, cb], writes=[cv])
                            P.dve(lambda e, AS=AS, fc=fc, cv=cv: e.scalar_tensor_tensor(
                                cv[:], AS[:, 1:QG + 1], cw[:, 1, fc:fc + 1], cv[:], ALU.mult, ALU.add),
                                reads=[AS, cw, cv], writes=[cv])
                            P.dve(lambda e, AS=AS, fc=fc, cv=cv: e.scalar_tensor_tensor(
                                cv[:], AS[:, 0:QG], cw[:, 0, fc:fc + 1], cv[:], ALU.mult, ALU.add),
                                reads=[AS, cw, cv], writes=[cv])
                            P.act(lambda e, cv=cv, gl=gl: e.activation(gl[:], cv[:], AF.Gelu_apprx_tanh),
                                  reads=[cv], writes=[gl])
                            P.dve(lambda e, pv=pv, fc=fc, gl=gl: e.tensor_tensor(hT[:, fc, :], pv[:, 0:QG], gl[:], ALU.mult),
                                  reads=[pv, gl], writes=[hT])
                    wds = []
                    for hh in range(2):
                        wd = wdr.next()
                        P.load(wd, wd[:], wdv[:, :, hh * 512:(hh + 1) * 512], eng="pool")
                        wds.append(wd)
                    if g + 1 < NG:
                        nxt = f1(g + 1)
                    for hh in range(2):
                        wd = wds[hh]
                        for ti in range(QB):
                            py = ps[4 + (ti % 2)]
                            for fc in range(FC):
                                P.pe(lambda e, py=py, fc=fc, ti=ti, wd=wd: e.matmul(
                                    py[:], hT[:, fc, ti * 128:(ti + 1) * 128], wd[:, fc, :],
                                    start=(fc == 0), stop=(fc == FC - 1)), reads=[hT, wd], writes=[py])
                            y2 = y2s[ti]
                            P.act(lambda e, py=py, y2=y2, hh=hh: e.activation(y2[:, hh * 512:(hh + 1) * 512], py[:], AF.Copy),
                                  reads=[py], writes=[y2])
                            if hh == 1:
                                t = g * QB + ti
                                post_norm_residual([(y2, y2[:, 0:512]), (y2, y2[:, 512:1024])], xts[ti], GP2,
                                                   otr.next(), st2r.next(), cm05, junk, dst[t * 128:(t + 1) * 128, :])

        xsrc = x_in
        def want(n):
            return only is None or n in only
        for l in range(L):
            if want("mod"):
                phase_mod(l)
            if stop_after == "mod":
                break
            if want("proj"):
                phase_proj(l, xsrc)
            if stop_after == "proj":
                break
            if MERGE_MIX and want("ret") and want("gla") and only is None and stop_after not in ("ret", "sb"):
                with P.phase("mix%d" % l):
                    P.nested = 1
                    gens = [phase_ret(l), phase_gla(l)]
                    weights = [1, 3]
                    alive = [True, True]
                    while any(alive):
                        for gi, gen in enumerate(gens):
                            for _ in range(weights[gi]):
                                if alive[gi]:
                                    try:
                                        next(gen)
                                    except StopIteration:
                                        alive[gi] = False
                    P.nested = 0
                if want("sb"):
                    phase_sb(l)
            else:
                if want("ret"):
                    for _ in phase_ret(l):
                        pass
                if stop_after == "ret":
                    break
                if want("sb"):
                    phase_sb(l)
                if stop_after == "sb":
                    break
                if want("gla"):
                    for _ in phase_gla(l):
                        pass
            if stop_after == "mix":
                break
            if want("out"):
                phase_out(l, xsrc)
            if stop_after == "out":
                break
            dst = out if l == L - 1 else xbuf
            if want("ffn"):
                phase_ffn(l, dst)
            xsrc = xbuf
    dbg = dict(modd=modd, qrT=qrT, krT=krT, sqT=sqT, skT=skT, gqT=gqT, gkT=gkT, glrT=glrT, rvg=rvg, svd=svd,
               gvg=gvg, mixed=mixed, x1d=x1d, xbuf=xbuf, sbT=sbT)
    return nc, dbg


_CACHE = {}


def kernel(**inputs):
    x = np.ascontiguousarray(np.asarray(inputs["x"], dtype=np.float32))
    c = np.ascontiguousarray(np.asarray(inputs["c"], dtype=np.float32))
    B, S, _ = x.shape
    L = int(np.asarray(inputs["ada_w"]).shape[0])
    key = (S, L)
    if key not in _CACHE:
        _CACHE[key] = (build(S, L)[0], make_consts(S))
    nc, consts = _CACHE[key]
    shared = {k: np.ascontiguousarray(np.asarray(inputs[k], dtype=np.float32)) for k in WEIGHT_SHAPES(L)}
    shared["kpack"] = pack_consts(consts, S)
    n_cores = 8
    in_maps = []
    for core in range(n_cores):
        b = core % B
        m = dict(shared)
        m["x"] = x[b]
        m["c"] = c[b]
        in_maps.append(m)
    res = run_bass_kernel_spmd(nc, in_maps, core_ids=list(range(n_cores)))
    outs = [np.asarray(res.results[b]["out"], dtype=np.float32) for b in range(B)]
    return np.stack(outs, axis=0)
```

```python
from contextlib import ExitStack, contextmanager
import numpy as np
import concourse.bass as bass
import concourse.mybir as mybir
from concourse.bass_utils import run_bass_kernel_spmd

F32 = mybir.dt.float32
F32R = mybir.dt.float32r
BF16 = mybir.dt.bfloat16
ALU = mybir.AluOpType
AF = mybir.ActivationFunctionType
AX = mybir.AxisListType

D = 1024
KC = 8
DFF = 2816
FC = DFF // 128
INC = 3728
EPS = 1e-6
ENGS = ("pe", "act", "dve", "pool", "sp")
import os
RETCUT = int(os.environ.get("RETCUT", "9"))
MERGE_MIX = int(os.environ.get("MERGE_MIX", "0"))


class Sem:
    def __init__(self, h):
        self.h = h
        self.cnt = 0


class Buf:
    def __init__(self, t, name):
        self.t = t
        self.name = name
        self.last_w = None
        self.readers = []
        self.dsem = None
        self.ssem = None
        self.ssem_h = None

    def __getitem__(self, idx):
        return self.t[idx]


class Prog:
    def __init__(self, nc, stack):
        self.nc = nc
        self.gstack = stack
        self.stack = stack
        self.q = {e: [] for e in ENGS}
        self.waited = {e: {} for e in ENGS}
        self.bufs = []
        self.nsem = 0
        self.sem = {e: self._newsem("s" + e) for e in ("pe", "act", "dve", "pool")}
        self.mksem = self._newsem("mk")
        self.pname = "ph"
        self.nested = 0
        self.dpool = []
        self.swused = set()
        self.swpool = []
        self.swmark = self.gstack.enter_context(self.nc.sbuf_tensor("swmark", [1, 8], F32))
        self.nbuf = 0

    def _newsem(self, name):
        self.nsem += 1
        return Sem(self.gstack.enter_context(self.nc.semaphore(f"{name}_{self.nsem}")))

    def buf(self, shape, dtype=F32, name=None, psum=False):
        self.nbuf += 1
        name = f"{name or 'b'}_{self.nbuf}"
        if psum:
            t = self.gstack.enter_context(self.nc.psum_tensor(name, list(shape), dtype))
        else:
            t = self.stack.enter_context(self.nc.sbuf_tensor(name, list(shape), dtype))
        b = Buf(t, name)
        self.bufs.append(b)
        return b

    def _wait(self, eng, dep):
        if dep is None:
            return
        if dep[0] == "multi":
            for d in dep[1]:
                self._wait(eng, d)
            return
        sem, val, src = dep
        if src == eng and eng == "pe":
            return
        w = self.waited[eng]
        if w.get(id(sem), 0) >= val:
            return
        w[id(sem)] = val
        self.q[eng].append(("wait", sem.h, val))

    def _deps(self, eng, reads, writes):
        for b in reads:
            self._wait(eng, b.last_w)
        for b in writes:
            self._wait(eng, b.last_w)
            for r in b.readers:
                self._wait(eng, r)

    @staticmethod
    def _mark(tok, reads, writes):
        for b in reads:
            b.readers.append(tok)
        for b in writes:
            b.last_w = tok
            b.readers = []

    def op(self, eng, fn, reads=(), writes=()):
        self._deps(eng, reads, writes)
        s = self.sem[eng]
        s.cnt += 1
        self.q[eng].append(("op", fn, s.h, 1))
        self._mark((s, s.cnt, eng), reads, writes)

    def pe(self, fn, reads=(), writes=()):
        self.op("pe", fn, reads, writes)

    def act(self, fn, reads=(), writes=()):
        self.op("act", fn, reads, writes)

    def dve(self, fn, reads=(), writes=()):
        self.op("dve", fn, reads, writes)

    def pool(self, fn, reads=(), writes=()):
        self.op("pool", fn, reads, writes)

    def _dma(self, eng, fn, b, is_load):
        if b.dsem is None:
            b.dsem = self.dpool.pop() if self.dpool else self._newsem("d")
        if is_load:
            self._deps(eng, (), (b,))
        else:
            self._deps(eng, (b,), ())
        s = b.dsem
        s.cnt += 16
        self.q[eng].append(("op", fn, s.h, 16))
        tok = (s, s.cnt, "dma")
        if is_load:
            self._mark(tok, (), (b,))
        else:
            self._mark(tok, (b,), ())

    def load(self, b, out_ap, in_ap, eng="sp", **kw):
        fn = lambda e: e.dma_start(out=out_ap, in_=in_ap, **kw)
        if eng != "pool":
            self._dma(eng, fn, b, True)
            return
        if b.ssem is None:
            b.ssem = self.swpool.pop() if self.swpool else self._newsem("sw")
        self._deps("pool", (), (b,))
        s = b.ssem
        s.cnt += 16
        self.q["pool"].append(("op", fn, s.h, 16))
        self._mark((s, s.cnt, "dma"), (), (b,))

    def store(self, b, out_ap, in_ap, eng="sp", **kw):
        self._dma(eng, lambda e: e.dma_start(out=out_ap, in_=in_ap, **kw), b, False)

    def flush(self):
        for b in self.bufs:
            if b.dsem is not None and b.dsem.cnt > 0:
                self._wait("sp", (b.dsem, b.dsem.cnt, "dma"))
            if b.ssem is not None:
                self._wait("sp", (b.ssem, b.ssem.cnt, "dma"))
        q = self.q

        def play(lst):
            def body(e):
                for it in lst:
                    if it[0] == "wait":
                        e.wait_ge(it[1], it[2])
                    elif it[0] == "clear":
                        e.sem_clear(it[1])
                    elif it[0] == "seminc":
                        e.sem_inc(it[1], 1)
                    else:
                        it[1](e).then_inc(it[2], it[3])
            return body

        with self.nc.named_scope(self.pname), self.nc.Block() as blk:
            if q["sp"]:
                blk.sync(play(q["sp"]))
            if q["pe"]:
                blk.tensor(play(q["pe"]))
            if q["act"]:
                blk.scalar(play(q["act"]))
            if q["dve"]:
                blk.vector(play(q["dve"]))
            if q["pool"]:
                blk.gpsimd(play(q["pool"]))
        self.q = {e: [] for e in ENGS}
        self.waited = {e: {} for e in ENGS}
        for b in self.bufs:
            b.last_w = None
            b.readers = []

    @contextmanager
    def phase(self, name="ph"):
        if self.nested:
            yield
            return
        self.pname = name
        n0 = len(self.bufs)
        with ExitStack() as ps:
            self.stack = ps
            yield
            self.flush()
            for b in self.bufs[n0:]:
                if b.dsem is not None:
                    self.dpool.append(b.dsem)
                    b.dsem = None
                if b.ssem is not None:
                    self.swpool.append(b.ssem)
                    b.ssem = None
            del self.bufs[n0:]
        self.stack = self.gstack


class Ring:
    def __init__(self, bufs):
        self.bufs = bufs
        self.i = -1

    def next(self):
        self.i = (self.i + 1) % len(self.bufs)
        return self.bufs[self.i]


def make_consts(S):
    c = {}
    c["ident"] = np.eye(128, dtype=np.float32)
    r = np.arange(128)
    perm = np.zeros((128, 128), np.float32)
    for rp in range(128):
        src = (rp // 64) * 64 + ((rp % 64) + 32) % 64
        perm[src, rp] = 1.0
    c["perm"] = perm
    half = 32
    pos = np.arange(S, dtype=np.float32)
    freqs = (np.float32(10000.0) ** (-(np.arange(half, dtype=np.float32) / np.float32(half)))).astype(np.float32)
    ang = (pos[:, None] * freqs[None, :]).astype(np.float32).astype(np.float64)
    fi = (r % 64) % 32
    c["cosT"] = np.cos(ang[:, fi]).T.astype(np.float32).copy()
    sgn = np.where((r % 64) < 32, -1.0, 1.0)
    c["sinT"] = (np.sin(ang[:, fi]).T * sgn[:, None]).astype(np.float32).copy()
    lg = np.log1p(-(2.0 ** (-5.0 - np.arange(6, dtype=np.float64))))
    s = np.arange(128)[:, None]
    t = np.arange(128)[None, :]
    rmask = np.zeros((128, 6, 128), np.float64)
    for h in range(6):
        same = (s // 64) == (t // 64)
        earlier = (s // 64) < (t // 64)
        m = np.where(same, np.exp(lg[h] * np.abs(t - s)), 0.0) + np.where(earlier, np.exp(lg[h] * (t - s)), 0.0)
        rmask[:, h, :] = m / 8.0
    c["rmask"] = rmask.astype(np.float32)
    kd = np.zeros((128, 384), np.float64)
    for h in range(6):
        kd[:, h * 64:(h + 1) * 64] = (np.exp(lg[h] * (127 - np.arange(128))) / 8.0)[:, None]
    c["kd"] = kd.astype(np.float32)
    qd = np.zeros((128, 3, 128), np.float64)
    g128 = np.zeros((128, 3), np.float64)
    for p in range(3):
        for rr in range(128):
            h = 2 * p + rr // 64
            qd[rr, p, :] = np.exp(lg[h] * (np.arange(128) + 1.0))
            g128[rr, p] = np.exp(lg[h] * 128.0)
    c["qd"] = qd.astype(np.float32)
    c["g128"] = g128.astype(np.float32)
    c["ntri"] = np.where(s >= t, -1.0, 0.0).astype(np.float32)
    c["nones"] = -np.ones((128, 128), np.float32)
    c["smask"] = np.where(s < t, 1.0, 0.0).astype(np.float32)
    c["m1"] = np.where(s <= t, 1.0, 0.0).astype(np.float32)
    c["m2"] = np.where((s > t) & ((s // 64) == (t // 64)), 1.0, 0.0).astype(np.float32)
    return c


CONST_SHAPES = lambda S: {
    "ident": [128, 128], "perm": [128, 128], "cosT": [128, S], "sinT": [128, S],
    "rmask": [128, 6, 128], "kd": [128, 384], "qd": [128, 3, 128], "g128": [128, 3],
    "ntri": [128, 128], "nones": [128, 128], "smask": [128, 128], "m1": [128, 128], "m2": [128, 128],
}

def pack_consts(consts, S):
    return np.ascontiguousarray(np.concatenate(
        [consts[k].reshape(128, -1) for k in CONST_SHAPES(S)], axis=1).astype(np.float32))


WEIGHT_SHAPES = lambda L: {
    "ada_w": [L, D, 6 * D], "ada_b": [L, 6 * D], "pre_mix_g": [L, D], "post_mix_g": [L, D],
    "w_in": [L, D, INC], "gla_w2": [L, 16, 256], "gla_b": [L, 256], "ret_norm_g": [L, 384],
    "gla_norm_g": [L, 256], "w_out": [L, D, D], "pre_ffn_g": [L, D], "post_ffn_g": [L, D],
    "w_up": [L, D, 2 * DFF], "conv_w": [L, 3, DFF], "conv_b": [L, DFF], "w_down": [L, DFF, D],
}


def build(S, L, stop_after=None, debug=False, only=None):
    NT = S // 128
    QG = min(512, S)
    NG = S // QG
    QB = QG // 128
    nc = bass.Bass("TRN2", target_bir_lowering=False)

    def din(name, shape):
        return nc.dram_tensor(name, list(shape), F32, kind="ExternalInput").ap()

    def dscr(name, shape):
        return nc.dram_tensor(name, list(shape), F32, kind="ExternalOutput" if debug else "Internal").ap()

    x_in = din("x", [S, D])
    c_in = din("c", [D])
    W = {k: din(k, v) for k, v in WEIGHT_SHAPES(L).items()}
    cshapes = CONST_SHAPES(S)
    ctot = sum(int(np.prod(v[1:])) for v in cshapes.values())
    cpack = din("kpack", [128, ctot])
    C = {}
    coff = 0
    for k, v in cshapes.items():
        n = int(np.prod(v[1:]))
        a = cpack[:, coff:coff + n]
        if len(v) == 3:
            a = a.rearrange("p (a b) -> p a b", a=v[1])
        C[k] = a
        coff += n
    out = nc.dram_tensor("out", [S, D], F32, kind="ExternalOutput").ap()

    modd = dscr("modd", [L, 6 * D])
    qrT = dscr("qrT", [3, 128, S]); krT = dscr("krT", [3, 128, S])
    sqT = dscr("sqT", [3, 128, S]); skT = dscr("skT", [3, 128, S])
    gqT = dscr("gqT", [2, 128, S]); gkT = dscr("gkT", [2, 128, S])
    glrT = dscr("glrT", [16, S])
    rvg = dscr("rvg", [S, 768]); svd = dscr("svd", [S, 384]); gvg = dscr("gvg", [S, 512])
    mixed = dscr("mixed", [S, D])
    sbT = dscr("sbT", [3, 128, S])
    x1d = dscr("x1d", [S, D])
    xbuf = dscr("xbuf", [S, D])

    with ExitStack() as gst:
        P = Prog(nc, gst)
        ps = [P.buf([128, 512], F32, f"ps{i}", psum=True) for i in range(8)]

        def bc_row(ap_row):
            return ap_row.partition_broadcast(128)

        def rstd_from_ss(ss, tmp, rs, cm05, n):
            P.dve(lambda e: e.tensor_scalar(tmp[:], ss[:], 1.0 / n, EPS, ALU.mult, ALU.add),
                  reads=[ss], writes=[tmp])
            P.pool(lambda e: e.tensor_tensor(rs[:], tmp[:], cm05[:, 0:rs.t.shape[1]], ALU.pow),
                   reads=[tmp, cm05], writes=[rs])

        def phase_mod(l):
            with P.phase("mod%d" % l):
                cT = P.buf([128, KC], F32, "cT")
                sT = P.buf([128, KC], F32, "sT")
                adab = P.buf([1, 6 * D], F32, "adab")
                modrow = P.buf([1, 6 * D], F32, "modrow")
                war = Ring([P.buf([128, KC, 512], F32, f"wa{i}") for i in range(2)])
                P.load(cT, cT[:], c_in.rearrange("(kc p) -> p kc", p=128), allow_slow_non_contiguous=True)
                P.load(adab, adab[:], W["ada_b"][l:l + 1, :])
                P.act(lambda e: e.activation(sT[:], cT[:], AF.Silu), reads=[cT], writes=[sT])
                awv = W["ada_w"][l].rearrange("(kc p) n -> p kc n", p=128)
                for cg in range(12):
                    wa = war.next()
                    P.load(wa, wa[:], awv[:, :, cg * 512:(cg + 1) * 512])
                    pz = ps[cg % 2]
                    for kc in range(KC):
                        P.pe(lambda e, pz=pz, wa=wa, kc=kc: e.matmul(
                            pz[0:1, :], sT[:, kc:kc + 1], wa[:, kc, :], start=(kc == 0), stop=(kc == KC - 1)),
                            reads=[sT, wa], writes=[pz])
                    P.dve(lambda e, pz=pz, cg=cg: e.tensor_tensor(
                        modrow[:, cg * 512:(cg + 1) * 512], pz[0:1, :], adab[:, cg * 512:(cg + 1) * 512], ALU.add),
                        reads=[pz, adab], writes=[modrow])
                P.store(modrow, modd[l:l + 1, :], modrow[:])

        def load_mod_tile(dst, l, idx):
            P.load(dst, dst[:], bc_row(modd[l, idx * D:(idx + 1) * D]))

        def norm_mod_transpose(xsrc, t, xt, hn, G, SH, xnT, col0, ident, st, cm05, junk):
            P.load(xt, xt[:], xsrc[t * 128:(t + 1) * 128, :])
            P.act(lambda e: e.activation(junk[:], xt[:], AF.Square, accum_out=st[:, 0:1]),
                  reads=[xt], writes=[junk, st])
            P.dve(lambda e: e.tensor_scalar(st[:, 1:2], st[:, 0:1], 1.0 / D, EPS, ALU.mult, ALU.add),
                  reads=[st], writes=[st])
            P.pool(lambda e: e.tensor_tensor(st[:, 2:3], st[:, 1:2], cm05[:, 0:1], ALU.pow),
                   reads=[st, cm05], writes=[st])
            P.dve(lambda e: e.scalar_tensor_tensor(hn[:], xt[:], st[:, 2:3], G[:], ALU.mult, ALU.mult),
                  reads=[xt, st, G], writes=[hn])
            P.dve(lambda e: e.tensor_tensor(hn[:], hn[:], SH[:], ALU.add), reads=[hn, SH], writes=[hn])
            for half in range(2):
                pt = ps[6 + half]
                for j in range(4):
                    kc = half * 4 + j
                    P.pe(lambda e, pt=pt, j=j, kc=kc: e.transpose(
                        pt[:, j * 128:(j + 1) * 128], hn[:, kc * 128:(kc + 1) * 128], ident[:]),
                        reads=[hn, ident], writes=[pt])
                P.act(lambda e, pt=pt, half=half: e.activation(
                    xnT[:, half * 4:(half + 1) * 4, col0:col0 + 128],
                    pt[:].rearrange("p (a b) -> p a b", a=4), AF.Copy), reads=[pt], writes=[xnT])

        def phase_proj(l, xsrc):
            with P.phase("proj%d" % l):
                ident = P.buf([128, 128], F32, "ident")
                perm = P.buf([128, 128], F32R, "perm")
                cosT = P.buf([128, S], F32, "cosT")
                sinT = P.buf([128, S], F32, "sinT")
                G1 = P.buf([128, D], F32, "G1")
                SH1 = P.buf([128, D], F32, "SH1")
                gtmp = P.buf([128, D], F32, "gtmp")
                cm05 = P.buf([128, 8], F32, "cm05")
                junk = P.buf([128, D], F32, "junk")
                xtr = Ring([P.buf([128, D], F32, f"xt{i}") for i in range(4)])
                hnr = Ring([P.buf([128, D], F32, f"hn{i}") for i in range(4)])
                str_ = Ring([P.buf([128, 4], F32, f"st{i}") for i in range(4)])
                xnTr = Ring([P.buf([128, KC, QG], F32R, f"xnT{i}") for i in range(2)])
                wfr = Ring([P.buf([128, KC, 512], F32R, f"wf{i}") for i in range(2)])
                wtr = Ring([P.buf([128, KC, 512], F32R, f"wt{i}") for i in range(2)])
                evr = Ring([P.buf([128, QG], F32R, f"ev{i}") for i in range(2)])
                t1 = P.buf([128, QG], F32, "t1")
                t2 = P.buf([128, QG], F32, "t2")
                resr = Ring([P.buf([128, QG], F32, f"res{i}") for i in range(2)])
                tokr = Ring([P.buf([128, 512], F32, f"tok{i}") for i in range(2)])

                P.load(ident, ident[:], C["ident"])
                P.load(perm, perm[:], C["perm"], eng="pool")
                P.load(cosT, cosT[:], C["cosT"])
                P.load(sinT, sinT[:], C["sinT"])
                P.pool(lambda e: e.memset(cm05[:], -0.5), writes=[cm05])
                P.load(G1, G1[:], bc_row(W["pre_mix_g"][l]))
                load_mod_tile(gtmp, l, 1)
                load_mod_tile(SH1, l, 0)
                P.dve(lambda e: e.scalar_tensor_tensor(G1[:], gtmp[:], 1.0, G1[:], ALU.add, ALU.mult),
                      reads=[gtmp, G1], writes=[G1])

                winv = W["w_in"][l].rearrange("(kc p) n -> p kc n", p=128)
                fm = []
                def sub(kind, dsts, base):
                    return [(128 * i, 128, kind, d, 0) for i, d in enumerate(dsts)]
                fm.append((0, 512, [(0, 128, "rot", qrT[0], 0), (128, 128, "rot", qrT[1], 0), (256, 128, "rot", qrT[2], 0),
                                    (384, 128, "rot", krT[0], 0)]))
                fm.append((512, 256, [(0, 128, "rot", krT[1], 0), (128, 128, "rot", krT[2], 0)]))
                fm.append((1536, 512, [(0, 128, "scale", sqT[0], 0), (128, 128, "scale", sqT[1], 0),
                                       (256, 128, "scale", sqT[2], 0), (384, 128, "plain", skT[0], 0)]))
                fm.append((2048, 256, [(0, 128, "plain", skT[1], 0), (128, 128, "plain", skT[2], 0)]))
                fm.append((2688, 512, [(0, 128, "plain", gqT[0], 0), (128, 128, "plain", gqT[1], 0),
                                       (256, 128, "plain", gkT[0], 0), (384, 128, "plain", gkT[1], 0)]))
                fm.append((3600, 128, [(0, 128, "glr", glrT, 0)]))
                tm = [(768, 384, rvg, 0), (1152, 384, rvg, 384), (2304, 384, svd, 0), (3200, 512, gvg, 0)]

                zi = 0
                def f1(g):
                    xn = xnTr.next()
                    for ti in range(QB):
                        norm_mod_transpose(xsrc, g * QB + ti, xtr.next(), hnr.next(), G1, SH1, xn, ti * 128,
                                           ident, str_.next(), cm05, junk)
                    return xn

                xn_next = f1(0)
                for g in range(NG):
                    gs = slice(g * QG, (g + 1) * QG)
                    xnT = xn_next
                    for (c0, ncol, subs) in fm:
                        wf = wfr.next()
                        P.load(wf, wf[:, :, 0:ncol], winv[:, :, c0:c0 + ncol], eng="pool")
                        for (co, m, kind, dst, _r0) in subs:
                            pz = ps[zi % 3]
                            zi += 1
                            for kc in range(KC):
                                P.pe(lambda e, pz=pz, wf=wf, kc=kc, m=m, co=co, xnT=xnT: e.matmul(
                                    pz[0:m, 0:QG], wf[:, kc, co:co + m], xnT[:, kc, :], start=(kc == 0), stop=(kc == KC - 1)),
                                    reads=[wf, xnT], writes=[pz])
                            ev = evr.next()
                            sc = 0.125 if kind == "scale" else 1.0
                            P.act(lambda e, pz=pz, ev=ev, m=m, sc=sc: e.activation(
                                ev[0:m, :], pz[0:m, 0:QG], AF.Copy, scale=sc), reads=[pz], writes=[ev])
                            if kind == "rot":
                                p2 = ps[3 + (zi % 2)]
                                P.pe(lambda e, p2=p2, ev=ev: e.matmul(p2[:, 0:QG], perm[:], ev[:], start=True, stop=True),
                                     reads=[perm, ev], writes=[p2])
                                P.dve(lambda e, ev=ev, gs=gs: e.tensor_tensor(t1[:], ev[:].bitcast(F32), cosT[:, gs], ALU.mult),
                                      reads=[ev, cosT], writes=[t1])
                                P.dve(lambda e, p2=p2, gs=gs: e.tensor_tensor(t2[:], p2[:, 0:QG], sinT[:, gs], ALU.mult),
                                      reads=[p2, sinT], writes=[t2])
                                res = resr.next()
                                P.dve(lambda e, res=res: e.tensor_tensor(res[:], t1[:], t2[:], ALU.add),
                                      reads=[t1, t2], writes=[res])
                                P.store(res, dst[:, gs], res[:])
                            elif kind == "glr":
                                P.store(ev, dst[:, gs], ev[112:128, :].bitcast(F32))
                            else:
                                P.store(ev, dst[0:m, gs], ev[0:m, :].bitcast(F32))
                    if g + 1 < NG:
                        xn_next = f1(g + 1)
                    for (c0, n, dst, dc0) in tm:
                        wt = wtr.next()
                        P.load(wt, wt[:, :, 0:n], winv[:, :, c0:c0 + n], eng="pool")
                        for ti in range(QB):
                            t = g * QB + ti
                            pz = ps[zi % 3]
                            zi += 1
                            for kc in range(KC):
                                P.pe(lambda e, pz=pz, wt=wt, kc=kc, n=n, ti=ti, xnT=xnT: e.matmul(
                                    pz[:, 0:n], xnT[:, kc, ti * 128:(ti + 1) * 128], wt[:, kc, 0:n],
                                    start=(kc == 0), stop=(kc == KC - 1)), reads=[wt, xnT], writes=[pz])
                            tk = tokr.next()
                            P.act(lambda e, pz=pz, tk=tk, n=n: e.activation(tk[:, 0:n], pz[:, 0:n], AF.Copy),
                                  reads=[pz], writes=[tk])
                            P.store(tk, dst[t * 128:(t + 1) * 128, dc0:dc0 + n], tk[:, 0:n])

        def phase_ret(l):
            with P.phase("ret%d" % l):
                ident = P.buf([128, 128], F32, "ident")
                rmask = P.buf([128, 6, 128], F32, "rmask")
                kd = P.buf([128, 384], F32, "kd")
                qd = P.buf([128, 3, 128], F32, "qd")
                g128 = P.buf([128, 3], F32, "g128")
                RG = P.buf([128, 384], F32, "RG")
                state = P.buf([128, 3, 64], F32, "rstate")
                cm05 = P.buf([128, 8], F32, "cm05")
                qz0r = Ring([P.buf([128, 3, 128], F32, f"qza{i}") for i in range(3)])
                qz1r = Ring([P.buf([128, 3, 128], F32, f"qzb{i}") for i in range(3)])
                ktr = Ring([P.buf([128, 3, 128], F32, f"kt{i}") for i in range(3)])
                vgr = Ring([P.buf([128, 768], F32, f"vg{i}") for i in range(3)])
                osb = P.buf([128, 384], F32, "osb")
                sq = P.buf([128, 384], F32, "sq")
                gate = P.buf([128, 384], F32, "gate")
                yr = Ring([P.buf([128, 384], F32, f"y{i}") for i in range(2)])
                s1 = P.buf([128, 6], F32, "s1")
                s2 = P.buf([128, 6], F32, "s2")
                mean = P.buf([128, 6], F32, "mean")
                msq = P.buf([128, 6], F32, "msq")
                var = P.buf([128, 6], F32, "var")
                rstd = P.buf([128, 6], F32, "rstd")
                psK, psS0, psS1, psO, psKV = ps[0], ps[1], ps[2], ps[3], ps[4]

                P.load(ident, ident[:], C["ident"])
                P.load(rmask, rmask[:], C["rmask"])
                P.load(kd, kd[:], C["kd"])
                P.load(qd, qd[:], C["qd"])
                P.load(g128, g128[:], C["g128"])
                P.load(RG, RG[:], bc_row(W["ret_norm_g"][l]))
                P.pool(lambda e: e.memset(cm05[:], -0.5), writes=[cm05])
                P.pool(lambda e: e.memset(state[:], 0.0), writes=[state])
                for b_ in qz0r.bufs:
                    P.pool(lambda e, b_=b_: e.memset(b_[64:128, :, :], 0.0), writes=[b_])
                for b_ in qz1r.bufs:
                    P.pool(lambda e, b_=b_: e.memset(b_[0:64, :, :], 0.0), writes=[b_])
                qv = qrT.rearrange("r p s -> p r s")
                kv = krT.rearrange("r p s -> p r s")
                def ret_loads(t):
                    ts_ = slice(t * 128, (t + 1) * 128)
                    qz0, qz1, kt, vg = qz0r.next(), qz1r.next(), ktr.next(), vgr.next()
                    P.load(qz0, qz0[0:64, :, :], qv[0:64, :, ts_])
                    P.load(qz1, qz1[64:128, :, :], qv[64:128, :, ts_])
                    P.load(kt, kt[:], kv[:, :, ts_])
                    P.load(vg, vg[:], rvg[ts_, :])
                    return qz0, qz1, kt, vg

                kdecr = Ring([P.buf([128, 384], F32, f"kdec{i}") for i in range(2)])
                qdeczr = Ring([[P.buf([128, 3, 128], F32, f"qdz{i}_{j}") for j in range(2)] for i in range(2)])
                scmr = Ring([P.buf([128, 768], F32, f"scm{i}") for i in range(2)])

                def stage_a(t, bufs):
                    qz0, qz1, kt, vg = bufs
                    qz = [qz0, qz1]
                    kdec, qdz, scm = kdecr.next(), qdeczr.next(), scmr.next()
                    for p in range(3):
                        P.pe(lambda e, p=p: e.transpose(psK[:, p * 128:(p + 1) * 128], kt[:, p, :], ident[:]),
                             reads=[kt, ident], writes=[psK])
                    P.dve(lambda e: e.tensor_tensor(kdec[:], psK[:, 0:384], kd[:], ALU.mult),
                          reads=[psK, kd], writes=[kdec])
                    for j in range(2):
                        P.dve(lambda e, j=j: e.tensor_tensor(qdz[j][:], qz[j][:], qd[:], ALU.mult),
                              reads=[qz[j], qd], writes=[qdz[j]])
                    for h in range(6):
                        p, j = h // 2, h % 2
                        pss = psS0 if h < 4 else psS1
                        P.pe(lambda e, pss=pss, h=h, p=p, j=j: e.matmul(
                            pss[:, (h % 4) * 128:(h % 4 + 1) * 128], kt[:, p, :], qz[j][:, p, :],
                            start=True, stop=True), reads=[kt, qz[j]], writes=[pss])
                    P.dve(lambda e: e.tensor_tensor(scm[:, 0:512], psS0[:], rmask[:, 0:4, :].rearrange("p a b -> p (a b)"),
                                                    ALU.mult), reads=[psS0, rmask], writes=[scm])
                    P.dve(lambda e: e.tensor_tensor(scm[:, 512:768], psS1[:, 0:256],
                                                    rmask[:, 4:6, :].rearrange("p a b -> p (a b)"), ALU.mult),
                          reads=[psS1, rmask], writes=[scm])
                    return kdec, qdz, scm

                def stage_b(t, bufs, aout):
                    qz0, qz1, kt, vg = bufs
                    kdec, qdz, scm = aout
                    ts_ = slice(t * 128, (t + 1) * 128)
                    for h in range(6):
                        p, j = h // 2, h % 2
                        P.pe(lambda e, h=h: e.matmul(
                            psO[:, h * 64:(h + 1) * 64], scm[:, h * 128:(h + 1) * 128], vg[:, h * 64:(h + 1) * 64],
                            start=True, stop=False), reads=[scm, vg], writes=[psO])
                        P.pe(lambda e, h=h, p=p, j=j: e.matmul(
                            psO[:, h * 64:(h + 1) * 64], qdz[j][:, p, :], state[:, p, :],
                            start=False, stop=True), reads=[qdz[j], state], writes=[psO])
                    for p in range(3):
                        P.pe(lambda e, p=p: e.matmul(
                            psKV[:, p * 128:(p + 1) * 128], kdec[:, p * 128:(p + 1) * 128], vg[:, p * 128:(p + 1) * 128],
                            start=True, stop=True), reads=[kdec, vg], writes=[psKV])
                    for h in range(6):
                        p, j = h // 2, h % 2
                        rows = slice(j * 64, (j + 1) * 64)
                        P.dve(lambda e, p=p, j=j, rows=rows: e.scalar_tensor_tensor(
                            state[rows, p, :], state[rows, p, :], g128[rows, p:p + 1],
                            psKV[rows, p * 128 + j * 64:p * 128 + (j + 1) * 64], ALU.mult, ALU.add),
                            reads=[state, g128, psKV], writes=[state])
                    P.act(lambda e: e.activation(osb[:], psO[:, 0:384], AF.Copy), reads=[psO], writes=[osb])
                    P.act(lambda e: e.activation(sq[:], osb[:], AF.Square), reads=[osb], writes=[sq])
                    P.act(lambda e: e.activation(gate[:], vg[:, 384:768], AF.Silu), reads=[vg], writes=[gate])
                    P.pool(lambda e: e.tensor_tensor(gate[:], gate[:], RG[:], ALU.mult), reads=[gate, RG], writes=[gate])
                    o3 = osb[:].rearrange("p (h d) -> p h d", h=6)
                    P.dve(lambda e: e.tensor_reduce(s1[:], o3, AX.X, ALU.add), reads=[osb], writes=[s1])
                    P.dve(lambda e: e.tensor_reduce(s2[:], sq[:].rearrange("p (h d) -> p h d", h=6), AX.X, ALU.add),
                          reads=[sq], writes=[s2])
                    P.dve(lambda e: e.tensor_scalar(mean[:], s1[:], 1.0 / 64, None, ALU.mult), reads=[s1], writes=[mean])
                    P.dve(lambda e: e.tensor_tensor(msq[:], mean[:], mean[:], ALU.mult), reads=[mean], writes=[msq])
                    P.dve(lambda e: e.scalar_tensor_tensor(var[:], s2[:], 1.0 / 64, msq[:], ALU.mult, ALU.subtract),
                          reads=[s2, msq], writes=[var])
                    P.dve(lambda e: e.tensor_scalar(var[:], var[:], EPS, None, ALU.add), reads=[var], writes=[var])
                    P.pool(lambda e: e.tensor_tensor(rstd[:], var[:], cm05[:, 0:6], ALU.pow),
                           reads=[var, cm05], writes=[rstd])
                    y = yr.next()
                    y3 = y[:].rearrange("p (h d) -> p h d", h=6)
                    P.dve(lambda e: e.tensor_tensor(y3, o3, mean[:].unsqueeze(2).broadcast_to([128, 6, 64]),
                                                    ALU.subtract), reads=[osb, mean], writes=[y])
                    P.dve(lambda e: e.tensor_tensor(y3, y3, rstd[:].unsqueeze(2).broadcast_to([128, 6, 64]),
                                                    ALU.mult), reads=[y, rstd], writes=[y])
                    P.pool(lambda e: e.tensor_tensor(y[:], y[:], gate[:], ALU.mult), reads=[y, gate], writes=[y])
                    P.store(y, mixed[ts_, 0:384], y[:])

                lb = {0: ret_loads(0)}
                if NT > 1:
                    lb[1] = ret_loads(1)
                ao = {0: stage_a(0, lb[0])}
                for t in range(NT):
                    if t + 2 < NT:
                        lb[t + 2] = ret_loads(t + 2)
                    if t + 1 < NT:
                        ao[t + 1] = stage_a(t + 1, lb[t + 1])
                    stage_b(t, lb[t], ao[t])
                    del lb[t], ao[t]
                    yield

        def phase_sb(l):
            with P.phase("sb%d" % l):
                ntri = P.buf([128, 128], F32R, "ntri")
                nones = P.buf([128, 128], F32R, "nones")
                zl = P.buf([128, 128], F32R, "zl")
                smask = P.buf([128, 128], F32, "smask")
                sv_all = P.buf([128, NT, 384], F32R, "sv_all")
                skr = Ring([P.buf([128, S], F32R, f"sk{i}") for i in range(2)])
                qg0r = Ring([P.buf([128, QG], F32R, f"qga{i}") for i in range(2)])
                qg1r = Ring([P.buf([128, QG], F32R, f"qgb{i}") for i in range(2)])
                Er = Ring([P.buf([128, QG], F32, f"E{i}") for i in range(3)])
                SPr = Ring([P.buf([128, QG], F32R, f"SP{i}") for i in range(4)])
                Ar = Ring([P.buf([128, QG], F32R, f"A{i}") for i in range(3)])
                RSr = Ring([P.buf([128, QG], F32R, f"RS{i}") for i in range(2)])
                ostr = Ring([P.buf([128, QG], F32, f"ost{i}") for i in range(2)])
                zr = Ring([ps[0], ps[1], ps[6]])
                ar = Ring([ps[2], ps[3]])
                por = Ring([ps[4], ps[5]])
                P.load(ntri, ntri[:], C["ntri"], eng="pool")
                P.load(nones, nones[:], C["nones"], eng="pool")
                P.load(smask, smask[:], C["smask"])
                P.pool(lambda e: e.memset(zl[:].bitcast(F32), 0.0), writes=[zl])
                for b_ in qg0r.bufs:
                    P.pool(lambda e, b_=b_: e.memset(b_[64:128, :].bitcast(F32), 0.0), writes=[b_])
                for b_ in qg1r.bufs:
                    P.pool(lambda e, b_=b_: e.memset(b_[0:64, :].bitcast(F32), 0.0), writes=[b_])
                P.load(sv_all, sv_all[:], svd.rearrange("(t p) c -> p t c", p=128), eng="pool")

                items = []
                for p in range(3):
                    for g in range(NG):
                        for j in range(2):
                            ctx = dict(p=p, g=g, j=j)
                            kbs = list(range(g * QB + QB - 1, -1, -1))
                            for i, kb in enumerate(kbs):
                                items.append((ctx, kb, i == 0, i == len(kbs) - 1))

                def ctx_begin(ctx):
                    p, g, j = ctx["p"], ctx["g"], ctx["j"]
                    if g == 0 and j == 0:
                        sk = skr.next()
                        P.load(sk, sk[:], skT[p], eng="pool")
                        ctx["sk_new"] = sk
                    if j == 0:
                        qga, qgb = qg0r.next(), qg1r.next()
                        P.load(qga, qga[0:64, :], sqT[p][0:64, g * QG:(g + 1) * QG], eng="pool")
                        P.load(qgb, qgb[64:128, :], sqT[p][64:128, g * QG:(g + 1) * QG], eng="pool")
                        cur["qga"], cur["qgb"] = qga, qgb
                        cur["ost"] = ostr.next()
                    if "sk_new" in ctx:
                        cur["sk"] = ctx["sk_new"]
                    ctx["sk"] = cur["sk"]
                    ctx["qg"] = cur["qga"] if j == 0 else cur["qgb"]
                    ctx["ost"] = cur["ost"]
                    ctx["RS"] = RSr.next()
                    ctx["po"] = por.next()
                    RS, po = ctx["RS"], ctx["po"]
                    P.pool(lambda e, RS=RS: e.memset(RS[:].bitcast(F32), 0.0), writes=[RS])
                    P.pe(lambda e, po=po: e.matmul(po[:, 0:QG], zl[:], sv_all[:, 0, 0:QG] if QG <= 384 else
                                                   sv_all[:, 0:2, :].rearrange("p a b -> p (a b)")[:, 0:QG],
                                                   start=True, stop=False), reads=[zl, sv_all], writes=[po])

                cur = {}

                def s1a(it):
                    ctx, kb, first, last = it
                    if first:
                        ctx_begin(ctx)
                    g = ctx["g"]
                    r = kb - g * QB
                    c0 = max(r, 0) * 128
                    cs = slice(c0, QG)
                    dg = slice(c0, c0 + 128)
                    pz, E = zr.next(), Er.next()
                    sk, qg = ctx["sk"], ctx["qg"]
                    P.pe(lambda e: e.matmul(pz[:, cs], sk[:, kb * 128:(kb + 1) * 128], qg[:, cs], start=True, stop=True),
                         reads=[sk, qg], writes=[pz])
                    P.act(lambda e: e.activation(E[:, cs], pz[:, cs], AF.Exp), reads=[pz], writes=[E])
                    return (E, r, cs, dg)

                def s1b(it, pre):
                    E, r, cs, dg = pre
                    SP = SPr.next()
                    P.act(lambda e: e.activation(SP[:, cs], E[:, cs], AF.Ln, bias=1.0), reads=[E], writes=[SP])
                    if r >= 0:
                        P.dve(lambda e: e.tensor_tensor(SP[:, dg], SP[:, dg].bitcast(F32), smask[:], ALU.mult),
                              reads=[SP, smask], writes=[SP])
                    return (SP, r, cs, dg)

                def rs_add(it, st):
                    ctx, kb, first, last = it
                    SP, r, cs, dg = st
                    RS = ctx["RS"]
                    P.dve(lambda e: e.tensor_tensor(RS[:, cs], RS[:, cs].bitcast(F32), SP[:, cs].bitcast(F32), ALU.add),
                          reads=[RS, SP], writes=[RS])

                def s2a(it, st):
                    ctx, kb, first, last = it
                    SP, r, cs, dg = st
                    sk, qg, RS = ctx["sk"], ctx["qg"], ctx["RS"]
                    pzb, A = ar.next(), Ar.next()
                    P.pe(lambda e: e.matmul(pzb[:, cs], sk[:, kb * 128:(kb + 1) * 128], qg[:, cs], start=True, stop=False),
                         reads=[sk, qg], writes=[pzb])
                    P.pe(lambda e: e.matmul(pzb[:, cs], ntri[:], SP[:, cs], start=False, stop=False),
                         reads=[ntri, SP], writes=[pzb])
                    P.pe(lambda e: e.matmul(pzb[:, cs], nones[:], RS[:, cs], start=False, stop=True),
                         reads=[nones, RS], writes=[pzb])
                    P.act(lambda e: e.activation(A[:, cs], pzb[:, cs], AF.Exp), reads=[pzb], writes=[A])
                    if r >= 0:
                        P.dve(lambda e: e.tensor_tensor(A[:, dg], A[:, dg].bitcast(F32), smask[:], ALU.mult),
                              reads=[A, smask], writes=[A])
                    return A

                def s2b(it, st, A):
                    ctx, kb, first, last = it
                    SP, r, cs, dg = st
                    p, j = ctx["p"], ctx["j"]
                    po = ctx["po"]
                    P.pe(lambda e: e.matmul(po[:, cs], sv_all[:, kb, p * 128:(p + 1) * 128], A[:, cs],
                                            start=False, stop=False), reads=[sv_all, A], writes=[po])
                    if last:
                        P.pe(lambda e: e.matmul(po[:, 0:QG], zl[:], sv_all[:, 0, 0:QG] if QG <= 384 else
                                                sv_all[:, 0:2, :].rearrange("p a b -> p (a b)")[:, 0:QG],
                                                start=False, stop=True), reads=[zl, sv_all], writes=[po])
                        ost = ctx["ost"]
                        rows = slice(j * 64, (j + 1) * 64)
                        P.act(lambda e: e.activation(ost[rows, :], po[rows, 0:QG], AF.Copy), reads=[po], writes=[ost])
                        if j == 1:
                            g = ctx["g"]
                            P.store(ost, sbT[p][:, g * QG:(g + 1) * QG], ost[:])

                n_it = len(items)
                pres, sts, As = {}, {}, {}
                pres[0] = s1a(items[0])
                if n_it > 1:
                    pres[1] = s1a(items[1])
                sts[0] = s1b(items[0], pres.pop(0))
                for i in range(n_it + 1):
                    if 1 <= i < n_it and not items[i][2]:
                        rs_add(items[i - 1], sts[i - 1])
                    if i + 2 < n_it:
                        pres[i + 2] = s1a(items[i + 2])
                    if i + 1 < n_it:
                        sts[i + 1] = s1b(items[i + 1], pres.pop(i + 1))
                    if i < n_it:
                        As[i] = s2a(items[i], sts[i])
                    if i >= 1:
                        s2b(items[i - 1], sts[i - 1], As[i - 1])
                        del sts[i - 1], As[i - 1]

        def phase_gla(l):
            with P.phase("gla%d" % l):
                ident = P.buf([128, 128], F32, "ident")
                m1 = P.buf([128, 128], F32, "m1")
                m2 = P.buf([128, 128], F32, "m2")
                ones = P.buf([128, 128], F32, "ones")
                w2 = P.buf([128, 256], F32, "w2")
                gb = P.buf([128, 2], F32, "gb")
                nb = P.buf([128, 2], F32, "nb")
                GG = P.buf([128, 256], F32, "GG")
                state = P.buf([128, 2, 64], F32, "gstate")
                cm05 = P.buf([128, 8], F32, "cm05")
                glr = P.buf([128, QG], F32, "glr")
                gq0r = Ring([P.buf([128, QG], F32, f"gqa{i}") for i in range(2)])
                gq1r = Ring([P.buf([128, QG], F32, f"gqb{i}") for i in range(2)])
                gkr = Ring([P.buf([128, QG], F32, f"gk{i}") for i in range(2)])
                e_ = P.buf([128, QG], F32, "e")
                sp = P.buf([128, QG], F32, "sp")
                bsp = P.buf([128, QG], F32, "bsp")
                EM = P.buf([128, QG], F32, "EM")
                EK = P.buf([128, QG], F32, "EK")
                nbl = P.buf([128, QB], F32, "nbl")
                vgr = Ring([P.buf([128, 512], F32, f"vg{i}") for i in range(3)])
                tmp2 = P.buf([128, 128], F32, "tmp2")
                osb = P.buf([128, 128], F32, "osb")
                sq = P.buf([128, 128], F32, "sq")
                gate = P.buf([128, 128], F32, "gate")
                s2 = P.buf([128, 2], F32, "s2")
                rstd = P.buf([128, 2], F32, "rstd")
                yr = Ring([P.buf([128, 128], F32, f"y{i}") for i in range(2)])
                psZ, psS, psT, psO, psKV = ps[5], ps[6], ps[7], ps[5], ps[7]

                P.load(ident, ident[:], C["ident"])
                P.load(m1, m1[:], C["m1"])
                P.load(m2, m2[:], C["m2"])
                P.pool(lambda e: e.memset(w2[:], 0.0), writes=[w2])
                P.pool(lambda e: e.memset(glr[:], 0.0), writes=[glr])
                P.load(w2, w2[0:16, :], W["gla_w2"][l])
                P.load(gb, gb[:], W["gla_b"][l].rearrange("(g p) -> p g", p=128), allow_slow_non_contiguous=True)
                P.load(GG, GG[:], bc_row(W["gla_norm_g"][l]))
                P.pool(lambda e: e.memset(cm05[:], -0.5), writes=[cm05])
                P.pool(lambda e: e.memset(ones[:], 1.0), writes=[ones])
                P.pool(lambda e: e.memset(state[:], 0.0), writes=[state])
                P.dve(lambda e: e.tensor_scalar(nb[:], gb[:], -1.0, None, ALU.mult), reads=[gb], writes=[nb])
                for b_ in gq0r.bufs:
                    P.pool(lambda e, b_=b_: e.memset(b_[64:128, :], 0.0), writes=[b_])
                for b_ in gq1r.bufs:
                    P.pool(lambda e, b_=b_: e.memset(b_[0:64, :], 0.0), writes=[b_])
                d1r = Ring([dict(qpz=[P.buf([128, QG], F32, f"qpz{i}_{j}") for j in range(2)],
                                 qmz=[P.buf([128, QG], F32, f"qmz{i}_{j}") for j in range(2)],
                                 km=P.buf([128, QG], F32, f"km{i}"), kp=P.buf([128, QG], F32, f"kp{i}"),
                                 ks=P.buf([128, QG], F32, f"ks{i}"), EP=P.buf([128, QG], F32, f"EP{i}"))
                            for i in range(2)])
                scr = Ring([P.buf([128, 2, 128], F32, f"sc{i}") for i in range(2)])
                kstr = Ring([P.buf([128, 128], F32, f"kst{i}") for i in range(2)])
                glr_loaded = {}

                def d1(g, cg):
                    gs = slice(g * QG, (g + 1) * QG)
                    if g not in glr_loaded:
                        P.load(glr, glr[0:16, :], glrT[:, gs])
                        glr_loaded[g] = True
                    o = d1r.next()
                    qpz, qmz, km, kp, ks, EP = o["qpz"], o["qmz"], o["km"], o["kp"], o["ks"], o["EP"]
                    gqa, gqb, gk = gq0r.next(), gq1r.next(), gkr.next()
                    gqz = [gqa, gqb]
                    P.load(gqa, gqa[0:64, :], gqT[cg][0:64, gs])
                    P.load(gqb, gqb[64:128, :], gqT[cg][64:128, gs])
                    P.load(gk, gk[:], gkT[cg][:, gs])
                    P.pe(lambda e: e.matmul(psZ[:, 0:QG], w2[:, cg * 128:(cg + 1) * 128], glr[:],
                                            start=True, stop=True), reads=[w2, glr], writes=[psZ])
                    P.act(lambda e: e.activation(e_[:], psZ[:, 0:QG], AF.Exp, bias=nb[:, cg:cg + 1], scale=-1.0),
                          reads=[psZ, nb], writes=[e_])
                    P.act(lambda e: e.activation(sp[:], e_[:], AF.Ln, bias=1.0), reads=[e_], writes=[sp])
                    for ti in range(QB):
                        sl = slice(ti * 128, (ti + 1) * 128)
                        P.dve(lambda e, sl=sl: e.tensor_tensor_scan(bsp[:, sl], ones[:], sp[:, sl], 0.0, ALU.mult, ALU.add),
                              reads=[ones, sp], writes=[bsp])
                    P.act(lambda e: e.activation(EP[:], bsp[:], AF.Exp, scale=-1.0 / 16), reads=[bsp], writes=[EP])
                    P.act(lambda e: e.activation(EM[:], bsp[:], AF.Exp, scale=1.0 / 16), reads=[bsp], writes=[EM])
                    P.dve(lambda e: e.tensor_scalar(
                        nbl[:], bsp[:].rearrange("p (t s) -> p t s", s=128)[:, :, 127], -1.0 / 16, None, ALU.mult),
                        reads=[bsp], writes=[nbl])
                    for ti in range(QB):
                        sl = slice(ti * 128, (ti + 1) * 128)
                        P.act(lambda e, sl=sl, ti=ti: e.activation(EK[:, sl], bsp[:, sl], AF.Exp,
                                                                   bias=nbl[:, ti:ti + 1], scale=1.0 / 16),
                              reads=[bsp, nbl], writes=[EK])
                    for j in range(2):
                        P.dve(lambda e, j=j: e.scalar_tensor_tensor(
                            qpz[j][:], gqz[j][:], 0.125, EP[:], ALU.mult, ALU.mult),
                            reads=[gqz[j], EP], writes=[qpz[j]])
                        P.dve(lambda e, j=j: e.scalar_tensor_tensor(
                            qmz[j][:], gqz[j][:], 0.125, EM[:], ALU.mult, ALU.mult),
                            reads=[gqz[j], EM], writes=[qmz[j]])
                    P.dve(lambda e: e.tensor_tensor(km[:], gk[:], EM[:], ALU.mult), reads=[gk, EM], writes=[km])
                    P.dve(lambda e: e.tensor_tensor(kp[:], gk[:], EP[:], ALU.mult), reads=[gk, EP], writes=[kp])
                    P.dve(lambda e: e.tensor_tensor(ks[:], gk[:], EK[:], ALU.mult), reads=[gk, EK], writes=[ks])
                    return o

                def gla_vg(g, ti):
                    t = g * QB + ti
                    vg = vgr.next()
                    P.load(vg, vg[:], gvg[t * 128:(t + 1) * 128, :])
                    return vg

                def stage_a(o, ti):
                    qpz, qmz, km, kp, ks = o["qpz"], o["qmz"], o["km"], o["kp"], o["ks"]
                    sl = slice(ti * 128, (ti + 1) * 128)
                    sc, kst = scr.next(), kstr.next()
                    for j in range(2):
                        P.pe(lambda e, j=j: e.matmul(
                            psS[:, j * 256:j * 256 + 128], km[:, sl], qpz[j][:, sl], start=True, stop=True),
                            reads=[km, qpz[j]], writes=[psS])
                        P.pe(lambda e, j=j: e.matmul(
                            psS[:, j * 256 + 128:j * 256 + 256], kp[:, sl], qmz[j][:, sl], start=True, stop=True),
                            reads=[kp, qmz[j]], writes=[psS])
                    for j in range(2):
                        P.dve(lambda e, j=j: e.tensor_tensor(sc[:, j, :], psS[:, j * 256:j * 256 + 128], m1[:], ALU.mult),
                              reads=[psS, m1], writes=[sc])
                        P.dve(lambda e, j=j: e.tensor_tensor(tmp2[:], psS[:, j * 256 + 128:j * 256 + 256], m2[:], ALU.mult),
                              reads=[psS, m2], writes=[tmp2])
                        P.dve(lambda e, j=j: e.tensor_tensor(sc[:, j, :], sc[:, j, :], tmp2[:], ALU.add),
                              reads=[sc, tmp2], writes=[sc])
                    P.pe(lambda e: e.transpose(psT[:, 0:128], ks[:, sl], ident[:]), reads=[ks, ident], writes=[psT])
                    P.act(lambda e: e.activation(kst[:], psT[:, 0:128], AF.Copy), reads=[psT], writes=[kst])
                    return sc, kst

                def stage_b(o, g, cg, ti, vg, aout):
                    qpz, EP = o["qpz"], o["EP"]
                    sc, kst = aout
                    t = g * QB + ti
                    sl = slice(ti * 128, (ti + 1) * 128)
                    ts_ = slice(t * 128, (t + 1) * 128)
                    for j in range(2):
                        h = 2 * cg + j
                        P.pe(lambda e, j=j, h=h: e.matmul(
                            psO[:, j * 64:(j + 1) * 64], sc[:, j, :], vg[:, h * 64:(h + 1) * 64],
                            start=True, stop=False), reads=[sc, vg], writes=[psO])
                        P.pe(lambda e, j=j: e.matmul(
                            psO[:, j * 64:(j + 1) * 64], qpz[j][:, sl], state[:, cg, :],
                            start=False, stop=True), reads=[qpz[j], state], writes=[psO])
                    P.pe(lambda e: e.matmul(psKV[:, 0:128], kst[:], vg[:, cg * 128:(cg + 1) * 128],
                                            start=True, stop=True), reads=[kst, vg], writes=[psKV])
                    for j in range(2):
                        rows = slice(j * 64, (j + 1) * 64)
                        lc = ti * 128 + 127
                        P.dve(lambda e, j=j, rows=rows, lc=lc: e.scalar_tensor_tensor(
                            state[rows, cg, :], state[rows, cg, :], EP[rows, lc:lc + 1],
                            psKV[rows, j * 64:(j + 1) * 64], ALU.mult, ALU.add),
                            reads=[state, EP, psKV], writes=[state])
                    P.act(lambda e: e.activation(osb[:], psO[:, 0:128], AF.Copy), reads=[psO], writes=[osb])
                    P.act(lambda e: e.activation(sq[:], osb[:], AF.Square), reads=[osb], writes=[sq])
                    P.act(lambda e: e.activation(gate[:], vg[:, 256 + cg * 128:256 + (cg + 1) * 128], AF.Silu),
                          reads=[vg], writes=[gate])
                    P.pool(lambda e: e.tensor_tensor(gate[:], gate[:], GG[:, cg * 128:(cg + 1) * 128], ALU.mult),
                           reads=[gate, GG], writes=[gate])
                    P.dve(lambda e: e.tensor_reduce(s2[:], sq[:].rearrange("p (h d) -> p h d", h=2), AX.X, ALU.add),
                          reads=[sq], writes=[s2])
                    P.dve(lambda e: e.tensor_scalar(s2[:], s2[:], 1.0 / 64, EPS, ALU.mult, ALU.add),
                          reads=[s2], writes=[s2])
                    P.pool(lambda e: e.tensor_tensor(rstd[:], s2[:], cm05[:, 0:2], ALU.pow),
                           reads=[s2, cm05], writes=[rstd])
                    y = yr.next()
                    P.dve(lambda e: e.tensor_tensor(
                        y[:].rearrange("p (h d) -> p h d", h=2), osb[:].rearrange("p (h d) -> p h d", h=2),
                        rstd[:].unsqueeze(2).broadcast_to([128, 2, 64]), ALU.mult),
                        reads=[osb, rstd], writes=[y])
                    P.pool(lambda e: e.tensor_tensor(y[:], y[:], gate[:], ALU.mult), reads=[y, gate], writes=[y])
                    P.store(y, mixed[ts_, 768 + cg * 128:768 + (cg + 1) * 128], y[:])

                seq = [(g, cg, ti) for g in range(NG) for cg in range(2) for ti in range(QB)]
                ctxs = {}

                def ctx_of(g, cg):
                    if (g, cg) not in ctxs:
                        ctxs[(g, cg)] = d1(g, cg)
                    return ctxs[(g, cg)]

                n = len(seq)
                vgs = {k: gla_vg(seq[k][0], seq[k][2]) for k in range(min(2, n))}
                aos = {0: stage_a(ctx_of(seq[0][0], seq[0][1]), seq[0][2])}
                for k in range(n):
                    g, cg, ti = seq[k]
                    if k + 2 < n:
                        vgs[k + 2] = gla_vg(seq[k + 2][0], seq[k + 2][2])
                    if k + 1 < n:
                        g1, cg1, ti1 = seq[k + 1]
                        aos[k + 1] = stage_a(ctx_of(g1, cg1), ti1)
                    stage_b(ctx_of(g, cg), g, cg, ti, vgs[k], aos[k])
                    del vgs[k], aos[k]
                    yield

        def post_norm_residual(pys, xt, GP, o, st, cm05, junk, dst_rows):
            for h in range(2):
                P.act(lambda e, h=h: e.activation(junk[:, 0:512], pys[h][1], AF.Square, accum_out=st[:, h:h + 1]),
                      reads=[pys[h][0]], writes=[junk, st])
            P.dve(lambda e: e.tensor_tensor(st[:, 2:3], st[:, 0:1], st[:, 1:2], ALU.add), reads=[st], writes=[st])
            P.dve(lambda e: e.tensor_scalar(st[:, 2:3], st[:, 2:3], 1.0 / D, EPS, ALU.mult, ALU.add),
                  reads=[st], writes=[st])
            P.pool(lambda e: e.tensor_tensor(st[:, 3:4], st[:, 2:3], cm05[:, 0:1], ALU.pow),
                   reads=[st, cm05], writes=[st])
            for h in range(2):
                P.dve(lambda e, h=h: e.scalar_tensor_tensor(
                    o[:, h * 512:(h + 1) * 512], pys[h][1], st[:, 3:4], GP[:, h * 512:(h + 1) * 512],
                    ALU.mult, ALU.mult), reads=[pys[h][0], st, GP], writes=[o])
            P.dve(lambda e: e.tensor_tensor(o[:], o[:], xt[:], ALU.add), reads=[o, xt], writes=[o])
            P.store(o, dst_rows, o[:])

        def phase_out(l, xsrc):
            with P.phase("out%d" % l):
                ident = P.buf([128, 128], F32, "ident")
                wsb = P.buf([128, KC, D], F32R, "wout")
                GP = P.buf([128, D], F32, "GP1")
                gtmp = P.buf([128, D], F32, "gtmp")
                cm05 = P.buf([128, 8], F32, "cm05")
                junk = P.buf([128, 512], F32, "junk")
                mxr = Ring([P.buf([128, D], F32, f"mx{i}") for i in range(3)])
                xtr = Ring([P.buf([128, D], F32, f"xt{i}") for i in range(3)])
                mTr = Ring([P.buf([128, KC, 128], F32R, f"mT{i}") for i in range(3)])
                otr = Ring([P.buf([128, D], F32, f"ot{i}") for i in range(2)])
                str_ = Ring([P.buf([128, 4], F32, f"st{i}") for i in range(2)])
                P.load(ident, ident[:], C["ident"])
                P.load(wsb, wsb[:], W["w_out"][l].rearrange("(kc p) n -> p kc n", p=128), eng="pool")
                P.load(GP, GP[:], bc_row(W["post_mix_g"][l]))
                load_mod_tile(gtmp, l, 2)
                P.dve(lambda e: e.tensor_tensor(GP[:], GP[:], gtmp[:], ALU.mult), reads=[GP, gtmp], writes=[GP])
                P.pool(lambda e: e.memset(cm05[:], -0.5), writes=[cm05])
                def out_loads(t):
                    ts_ = slice(t * 128, (t + 1) * 128)
                    m, xx, mt = mxr.next(), xtr.next(), mTr.next()
                    P.load(m, m[:, 0:384], mixed[ts_, 0:384])
                    P.load(m, m[:, 768:1024], mixed[ts_, 768:1024])
                    P.load(xx, xx[:], xsrc[ts_, :])
                    P.load(mt, mt[:, 3:6, :], sbT.rearrange("r p s -> p r s")[:, :, ts_], eng="pool")
                    return m, xx, mt

                def out_a(t, bufs):
                    m, xx, mt = bufs
                    pt = ps[6 + (t % 2)]
                    for j, kc in enumerate((0, 1, 2)):
                        P.pe(lambda e, j=j, kc=kc: e.transpose(
                            pt[:, j * 128:(j + 1) * 128], m[:, kc * 128:(kc + 1) * 128], ident[:]),
                            reads=[m, ident], writes=[pt])
                    P.act(lambda e: e.activation(
                        mt[:, 0:3, :], pt[:, 0:384].rearrange("p (a b) -> p a b", a=3), AF.Copy),
                        reads=[pt], writes=[mt])
                    pt2 = ps[4 + (t % 2)]
                    for j, kc in enumerate((6, 7)):
                        P.pe(lambda e, j=j, kc=kc: e.transpose(
                            pt2[:, j * 128:(j + 1) * 128], m[:, kc * 128:(kc + 1) * 128], ident[:]),
                            reads=[m, ident], writes=[pt2])
                    P.act(lambda e: e.activation(
                        mt[:, 6:8, :], pt2[:, 0:256].rearrange("p (a b) -> p a b", a=2), AF.Copy),
                        reads=[pt2], writes=[mt])
                    pys = [ps[(t % 2) * 2], ps[(t % 2) * 2 + 1]]
                    for h in range(2):
                        for kc in range(KC):
                            P.pe(lambda e, py=pys[h], kc=kc, h=h: e.matmul(
                                py[:], mt[:, kc, :], wsb[:, kc, h * 512:(h + 1) * 512],
                                start=(kc == 0), stop=(kc == KC - 1)), reads=[mt, wsb], writes=[pys[h]])
                    return pys

                lb = {0: out_loads(0)}
                if NT > 1:
                    lb[1] = out_loads(1)
                pyo = {0: out_a(0, lb[0])}
                for t in range(NT):
                    ts_ = slice(t * 128, (t + 1) * 128)
                    if t + 2 < NT:
                        lb[t + 2] = out_loads(t + 2)
                    if t + 1 < NT:
                        pyo[t + 1] = out_a(t + 1, lb[t + 1])
                    post_norm_residual([(b_, b_[:]) for b_ in pyo[t]], lb[t][1], GP, otr.next(), str_.next(),
                                       cm05, junk, x1d[ts_, :])
                    del lb[t], pyo[t]

        def phase_ffn(l, dst):
            with P.phase("ffn%d" % l):
                ident = P.buf([128, 128], F32, "ident")
                G2 = P.buf([128, D], F32, "G2")
                SH2 = P.buf([128, D], F32, "SH2")
                GP2 = P.buf([128, D], F32, "GP2")
                cm05 = P.buf([128, 8], F32, "cm05")
                junk = P.buf([128, D], F32, "junk")
                gtmp = junk
                cw = P.buf([128, 3, FC], F32, "cw")
                cb = P.buf([128, FC], F32, "cb")
                halo = P.buf([128, FC, 2], F32, "halo")
                xtr = Ring([P.buf([128, D], F32, f"xt{i}") for i in range(2 * QB)])
                hnr = Ring([P.buf([128, D], F32, f"hn{i}") for i in range(2)])
                str_ = Ring([P.buf([128, 4], F32, f"st{i}") for i in range(4)])
                st2r = Ring([P.buf([128, 4], F32, f"su{i}") for i in range(2)])
                xnTr = Ring([P.buf([128, KC, QG], BF16, f"xn2T{i}") for i in range(2)])
                hT = P.buf([128, FC, QG], BF16, "hT")
                NSC = 4
                wavr = Ring([P.buf([128, KC, 2, NSC * 128], BF16, f"wav{i}") for i in range(2)])
                wdr = Ring([P.buf([128, FC, 512], BF16, f"wd{i}") for i in range(2)])
                ASr = Ring([P.buf([128, QG + 2], F32, f"AS{i}") for i in range(2)])
                cvr = Ring([P.buf([128, QG], F32, f"cv{i}") for i in range(2)])
                glr_ = Ring([P.buf([128, QG], F32, f"gl{i}") for i in range(2)])
                otr = Ring([P.buf([128, D], F32, f"ot{i}") for i in range(2)])
                y2s = [P.buf([128, D], F32, f"y2s{i}") for i in range(QB)]
                P.load(ident, ident[:], C["ident"])
                P.load(G2, G2[:], bc_row(W["pre_ffn_g"][l]))
                load_mod_tile(gtmp, l, 4)
                load_mod_tile(SH2, l, 3)
                P.dve(lambda e: e.scalar_tensor_tensor(G2[:], gtmp[:], 1.0, G2[:], ALU.add, ALU.mult),
                      reads=[gtmp, G2], writes=[G2])
                P.load(GP2, GP2[:], bc_row(W["post_ffn_g"][l]))
                load_mod_tile(gtmp, l, 5)
                P.dve(lambda e: e.tensor_tensor(GP2[:], GP2[:], gtmp[:], ALU.mult), reads=[GP2, gtmp], writes=[GP2])
                P.pool(lambda e: e.memset(cm05[:], -0.5), writes=[cm05])
                P.pool(lambda e: e.memset(halo[:], 0.0), writes=[halo])
                P.load(cw, cw[:], W["conv_w"][l].rearrange("i (fc p) -> p i fc", p=128), allow_slow_non_contiguous=True)
                P.load(cb, cb[:], W["conv_b"][l].rearrange("(fc p) -> p fc", p=128), allow_slow_non_contiguous=True)
                wuv = W["w_up"][l].rearrange("(kc p) n -> p kc n", p=128)
                wdv = W["w_down"][l].rearrange("(fc p) n -> p fc n", p=128)
                zi = 0

                def f1(g):
                    xn = xnTr.next()
                    xts = []
                    for ti in range(QB):
                        xt = xtr.next()
                        xts.append(xt)
                        norm_mod_transpose(x1d, g * QB + ti, xt, hnr.next(), G2, SH2, xn, ti * 128,
                                           ident, str_.next(), cm05, junk)
                    return xn, xts

                nxt = f1(0)
                for g in range(NG):
                    xnT, xts = nxt
                    for sc in range(0, FC, NSC):
                        nsc = min(NSC, FC - sc)
                        wav = wavr.next()
                        P.load(wav, wav[:, :, 0, 0:nsc * 128], wuv[:, :, sc * 128:(sc + nsc) * 128], eng="pool")
                        P.load(wav, wav[:, :, 1, 0:nsc * 128], wuv[:, :, DFF + sc * 128:DFF + (sc + nsc) * 128], eng="pool")
                        for fi in range(nsc):
                            fc = sc + fi
                            pa = ps[zi % 2]
                            pv = ps[2 + zi % 2]
                            zi += 1
                            for kc in range(KC):
                                P.pe(lambda e, pa=pa, wav=wav, kc=kc, fi=fi, xnT=xnT: e.matmul(
                                    pa[:, 0:QG], wav[:, kc, 0, fi * 128:(fi + 1) * 128], xnT[:, kc, :],
                                    start=(kc == 0), stop=(kc == KC - 1)), reads=[wav, xnT], writes=[pa])
                            for kc in range(KC):
                                P.pe(lambda e, pv=pv, wav=wav, kc=kc, fi=fi, xnT=xnT: e.matmul(
                                    pv[:, 0:QG], wav[:, kc, 1, fi * 128:(fi + 1) * 128], xnT[:, kc, :],
                                    start=(kc == 0), stop=(kc == KC - 1)), reads=[wav, xnT], writes=[pv])
                            AS, cv, gl = ASr.next(), cvr.next(), glr_.next()
                            P.act(lambda e, AS=AS, pa=pa: e.activation(AS[:, 2:QG + 2], pa[:, 0:QG], AF.Copy),
                                  reads=[pa], writes=[AS])
                            P.dve(lambda e, AS=AS, fc=fc: e.tensor_copy(AS[:, 0:2], halo[:, fc, :]),
                                  reads=[halo, AS], writes=[AS])
                            P.dve(lambda e, AS=AS, fc=fc: e.tensor_copy(halo[:, fc, :], AS[:, QG:QG + 2]),
                                  reads=[AS], writes=[halo])
                            P.dve(lambda e, AS=AS, fc=fc, cv=cv: e.tensor_scalar(
                                cv[:], AS[:, 2:QG + 2], cw[:, 2, fc:fc + 1], cb[:, fc:fc + 1], ALU.mult, ALU.add),
                                reads=[AS, cw, cb], writes=[cv])
                            P.dve(lambda e, AS=AS, fc=fc, cv=cv: e.scalar_tensor_tensor(
                                cv[:], AS[:, 1:QG + 1], cw[:, 1, fc:fc + 1], cv[:], ALU.mult, ALU.add),
                                reads=[AS, cw, cv], writes=[cv])
                            P.dve(lambda e, AS=AS, fc=fc, cv=cv: e.scalar_tensor_tensor(
                                cv[:], AS[:, 0:QG], cw[:, 0, fc:fc + 1], cv[:], ALU.mult, ALU.add),
                                reads=[AS, cw, cv], writes=[cv])
                            P.act(lambda e, cv=cv, gl=gl: e.activation(gl[:], cv[:], AF.Gelu_apprx_tanh),
                                  reads=[cv], writes=[gl])
                            P.dve(lambda e, pv=pv, fc=fc, gl=gl: e.tensor_tensor(hT[:, fc, :], pv[:, 0:QG], gl[:], ALU.mult),
                                  reads=[pv, gl], writes=[hT])
                    wds = []
                    for hh in range(2):
                        wd = wdr.next()
                        P.load(wd, wd[:], wdv[:, :, hh * 512:(hh + 1) * 512], eng="pool")
                        wds.append(wd)
                    if g + 1 < NG:
                        nxt = f1(g + 1)
                    for hh in range(2):
                        wd = wds[hh]
                        for ti in range(QB):
                            py = ps[4 + (ti % 2)]
                            for fc in range(FC):
                                P.pe(lambda e, py=py, fc=fc, ti=ti, wd=wd: e.matmul(
                                    py[:], hT[:, fc, ti * 128:(ti + 1) * 128], wd[:, fc, :],
                                    start=(fc == 0), stop=(fc == FC - 1)), reads=[hT, wd], writes=[py])
                            y2 = y2s[ti]
                            P.act(lambda e, py=py, y2=y2, hh=hh: e.activation(y2[:, hh * 512:(hh + 1) * 512], py[:], AF.Copy),
                                  reads=[py], writes=[y2])
                            if hh == 1:
                                t = g * QB + ti
                                post_norm_residual([(y2, y2[:, 0:512]), (y2, y2[:, 512:1024])], xts[ti], GP2,
                                                   otr.next(), st2r.next(), cm05, junk, dst[t * 128:(t + 1) * 128, :])

        xsrc = x_in
        def want(n):
            return only is None or n in only
        for l in range(L):
            if want("mod"):
                phase_mod(l)
            if stop_after == "mod":
                break
            if want("proj"):
                phase_proj(l, xsrc)
            if stop_after == "proj":
                break
            if MERGE_MIX and want("ret") and want("gla") and only is None and stop_after not in ("ret", "sb"):
                with P.phase("mix%d" % l):
                    P.nested = 1
                    gens = [phase_ret(l), phase_gla(l)]
                    weights = [1, 3]
                    alive = [True, True]
                    while any(alive):
                        for gi, gen in enumerate(gens):
                            for _ in range(weights[gi]):
                                if alive[gi]:
                                    try:
                                        next(gen)
                                    except StopIteration:
                                        alive[gi] = False
                    P.nested = 0
                if want("sb"):
                    phase_sb(l)
            else:
                if want("ret"):
                    for _ in phase_ret(l):
                        pass
                if stop_after == "ret":
                    break
                if want("sb"):
                    phase_sb(l)
                if stop_after == "sb":
                    break
                if want("gla"):
                    for _ in phase_gla(l):
                        pass
            if stop_after == "mix":
                break
            if want("out"):
                phase_out(l, xsrc)
            if stop_after == "out":
                break
            dst = out if l == L - 1 else xbuf
            if want("ffn"):
                phase_ffn(l, dst)
            xsrc = xbuf
    dbg = dict(modd=modd, qrT=qrT, krT=krT, sqT=sqT, skT=skT, gqT=gqT, gkT=gkT, glrT=glrT, rvg=rvg, svd=svd,
               gvg=gvg, mixed=mixed, x1d=x1d, xbuf=xbuf, sbT=sbT)
    return nc, dbg


_CACHE = {}


def kernel(**inputs):
    x = np.ascontiguousarray(np.asarray(inputs["x"], dtype=np.float32))
    c = np.ascontiguousarray(np.asarray(inputs["c"], dtype=np.float32))
    B, S, _ = x.shape
    L = int(np.asarray(inputs["ada_w"]).shape[0])
    key = (S, L)
    if key not in _CACHE:
        _CACHE[key] = (build(S, L)[0], make_consts(S))
    nc, consts = _CACHE[key]
    shared = {k: np.ascontiguousarray(np.asarray(inputs[k], dtype=np.float32)) for k in WEIGHT_SHAPES(L)}
    shared["kpack"] = pack_consts(consts, S)
    n_cores = 8
    in_maps = []
    for core in range(n_cores):
        b = core % B
        m = dict(shared)
        m["x"] = x[b]
        m["c"] = c[b]
        in_maps.append(m)
    res = run_bass_kernel_spmd(nc, in_maps, core_ids=list(range(n_cores)))
    outs = [np.asarray(res.results[b]["out"], dtype=np.float32) for b in range(B)]
    return np.stack(outs, axis=0)
```
